# Optimizing a Trainium2 kernel written in Bass

```python
import functools
import jax, jax.numpy as jnp
from jax import lax
import numpy as np

D_MODEL = 1024
BATCH = 2
SEQ = 8192
DEPTH = 4
DEC_BATCH = 128
DEC_SEQ = 4
PAST_LEN = 2048
PAGE_SIZE = 128

N_META = 16
N_GROUPS = 4
GROUP_W = D_MODEL // N_GROUPS
N_HEADS = 4
HEAD_DIM = GROUP_W // N_HEADS
GLA_DK = HEAD_DIM // 2
GLA_QK_W = N_HEADS * GLA_DK
GLA_RANK = 16
GLA_TAU = 16.0
GLA_CHUNK = 64
FOX_BLOCK = 128
CONV_C_W = 3
CONV_D_W = 4
LRU_C = 8.0
D_FF = 256 * ((8 * D_MODEL // 3 + 255) // 256)
EPS = 1e-6

SPLIT_WIDTHS = (GLA_QK_W, GLA_QK_W, GROUP_W, GROUP_W, GLA_RANK,
                GROUP_W, GROUP_W, GROUP_W, N_HEADS,
                GROUP_W, GROUP_W, GROUP_W,
                GROUP_W, GROUP_W)
D_IN = sum(SPLIT_WIDTHS)
SPLIT_IDX = tuple(int(i) for i in np.cumsum(SPLIT_WIDTHS)[:-1])

kernel_name = 'hymba_gla_fox_conv_rglru_step'


def rmsnorm(x, g):
    xf = x.astype(jnp.float32)
    y = xf * lax.rsqrt(jnp.mean(xf * xf, axis=-1, keepdims=True) + EPS)
    return (y * g.astype(jnp.float32)).astype(x.dtype)


def rms_unit(x):
    xf = x.astype(jnp.float32)
    return xf * lax.rsqrt(jnp.mean(xf * xf, axis=-1, keepdims=True) + EPS)


def swiglu(x, w_gate, w_up, w_down):
    return (jax.nn.silu(x @ w_gate) * (x @ w_up)) @ w_down


def causal_dwconv(x, buf, w):
    width = w.shape[0]
    T = x.shape[1]
    xp = jnp.concatenate([buf, x], axis=1)
    y = sum(w[j] * xp[:, j:j + T] for j in range(width))
    return y, xp[:, xp.shape[1] - (width - 1):]


def gla_chunked(q, k, v, log_a, s0, chunk):
    bsz, T, H, dk = q.shape
    n = T // chunk

    def to_chunks(t):
        return jnp.moveaxis(t.reshape((bsz, n, chunk) + t.shape[2:]), 1, 0)

    causal = jnp.tril(jnp.ones((chunk, chunk), bool))

    def step(S, inp):
        qi, ki, vi, ai = inp
        b = jnp.cumsum(ai, axis=1)
        b_last = b[:, -1:]
        q_t = qi * jnp.exp(b)
        k_t = ki * jnp.exp(-b)
        sc = jnp.where(causal, jnp.einsum('bthd,bshd->bhts', q_t, k_t), 0.0)
        o = jnp.einsum('bhts,bshv->bthv', sc, vi) + jnp.einsum('bthd,bhdv->bthv', q_t, S)
        k_end = ki * jnp.exp(b_last - b)
        S = S * jnp.exp(b_last[:, 0])[..., None] + jnp.einsum('bshd,bshv->bhdv', k_end, vi)
        return S, o

    S, o = lax.scan(step, s0, (to_chunks(q), to_chunks(k), to_chunks(v), to_chunks(log_a)))
    return jnp.moveaxis(o, 0, 1).reshape(bsz, T, H, v.shape[-1]), S


def fox_block(q, k, v, cq, ck, q_pos, k_pos):
    s = jnp.einsum('bthd,bshd->bhts', q, k) * (HEAD_DIM ** -0.5)
    s = s + (jnp.transpose(cq, (0, 2, 1))[:, :, :, None] - jnp.transpose(ck, (0, 2, 1))[:, :, None, :])
    s = jnp.where(q_pos[:, None] >= k_pos[None, :], s, -jnp.inf)
    p = jax.nn.softmax(s, axis=-1)
    return jnp.einsum('bhts,bshd->bthd', p, v)


def fox_prompt(q, k, v, log_f):
    bsz, L, H, D = q.shape
    c = jnp.cumsum(log_f, axis=1)
    pos = jnp.arange(L)
    o_meta = fox_block(q[:, :N_META], k[:, :N_META], v[:, :N_META], c[:, :N_META], c[:, :N_META],
                       pos[:N_META], pos[:N_META])
    n = (L - N_META) // FOX_BLOCK

    def blocks(t):
        return jnp.moveaxis(t[:, N_META:].reshape((bsz, n, FOX_BLOCK) + t.shape[2:]), 1, 0)

    o_real = lax.map(lambda blk: fox_block(blk[0], k, v, blk[1], c, blk[2], pos),
                     (blocks(q), blocks(c), pos[N_META:].reshape(n, FOX_BLOCK)))
    o_real = jnp.moveaxis(o_real, 0, 1).reshape(bsz, L - N_META, H, D)
    return jnp.concatenate([o_meta, o_real], axis=1)


def fox_sample(q, k, v, log_f, k_past, v_past, logf_past):
    P, T = k_past.shape[1], q.shape[1]
    k_all = jnp.concatenate([k_past.astype(jnp.float32), k], axis=1)
    v_all = jnp.concatenate([v_past.astype(jnp.float32), v], axis=1)
    c = jnp.cumsum(jnp.concatenate([logf_past.astype(jnp.float32), log_f], axis=1), axis=1)
    pos = jnp.arange(P + T)
    return fox_block(q, k_all, v_all, c[:, P:], c, pos[P:], pos)


def rglru(xc, w_r, b_r, w_i, b_i, lam, h0):
    bsz, T, _ = xc.shape
    xh = xc.reshape(bsz, T, N_HEADS, HEAD_DIM)
    r = jax.nn.sigmoid(jnp.einsum('bthi,hij->bthj', xh, w_r).reshape(bsz, T, GROUP_W) + b_r)
    ig = jax.nn.sigmoid(jnp.einsum('bthi,hij->bthj', xh, w_i).reshape(bsz, T, GROUP_W) + b_i)
    log_a = -LRU_C * r * jax.nn.softplus(-lam)
    a = jnp.exp(log_a)
    u = jnp.sqrt(-jnp.expm1(2.0 * log_a)) * (ig * xc)
    u = u.at[:, 0].add(a[:, 0] * h0)

    def combine(lhs, rhs):
        return lhs[0] * rhs[0], rhs[0] * lhs[1] + rhs[1]

    _, h = lax.associative_scan(combine, (a, u), axis=1)
    return h, h[:, -1]


def token_mixing(h, p, gla_s0, conv_c_buf, lru_h0, conv_d_buf, gla_segments, fox_attend):
    f32 = jnp.float32
    bsz, T, _ = h.shape
    z = (h @ p['w_in']).astype(f32)
    qa, ka, va, ga, ra, qb, kb, vb, fb, bc, cc, hc, gd, xd = jnp.split(z, SPLIT_IDX, axis=-1)

    def heads(t, d):
        return t.reshape(bsz, T, N_HEADS, d)

    log_alpha = jax.nn.log_sigmoid(ra @ p['w_alpha_up'] + p['b_alpha']) / GLA_TAU
    q_a, k_a = heads(qa * GLA_DK ** -0.5, GLA_DK), heads(ka, GLA_DK)
    v_a, l_a = heads(va, HEAD_DIM), heads(log_alpha, GLA_DK)
    S = gla_s0.astype(f32)
    outs = []
    start = 0
    for length, chunk in gla_segments:
        o, S = gla_chunked(q_a[:, start:start + length], k_a[:, start:start + length],
                           v_a[:, start:start + length], l_a[:, start:start + length], S, chunk)
        outs.append(o)
        start += length
    y_a = rms_unit(jnp.concatenate(outs, axis=1)).reshape(bsz, T, GROUP_W) * jax.nn.silu(ga)

    q_b, k_b, v_b = heads(qb, HEAD_DIM), heads(kb, HEAD_DIM), heads(vb, HEAD_DIM)
    log_f = jax.nn.log_sigmoid(fb + p['b_forget'])
    y_b = rms_unit(fox_attend(q_b, k_b, v_b, log_f).reshape(bsz, T, GROUP_W))

    conv_c_out, conv_c_new = causal_dwconv(cc * hc, conv_c_buf.astype(f32), p['conv_c_w'])
    y_c = rms_unit(bc * conv_c_out)

    xcd, conv_d_new = causal_dwconv(xd, conv_d_buf.astype(f32), p['conv_d_w'])
    h_seq, h_last = rglru(xcd + p['conv_d_b'], p['lru_w_r'], p['lru_b_r'], p['lru_w_i'], p['lru_b_i'],
                          p['lru_lambda'], lru_h0.astype(f32))
    y_d = rms_unit(h_seq * jax.nn.gelu(gd))

    y = jnp.concatenate([y_a, y_b, y_c, y_d], axis=-1) * p['g_norm']
    out = y.astype(h.dtype) @ p['w_out']
    return out, (k_b, v_b, log_f, S, conv_c_new, h_last, conv_d_new)


def layer(x, p, gla_s0, conv_c_buf, lru_h0, conv_d_buf, gla_segments, fox_attend):
    x = x + 0.5 * swiglu(rmsnorm(x, p['ln_ffn1']), p['ffn1_gate'], p['ffn1_up'], p['ffn1_down'])
    m, st = token_mixing(rmsnorm(x, p['ln_mix']), p, gla_s0, conv_c_buf, lru_h0, conv_d_buf,
                         gla_segments, fox_attend)
    x = x + m
    x = x + 0.5 * swiglu(rmsnorm(x, p['ln_ffn2']), p['ffn2_gate'], p['ffn2_up'], p['ffn2_down'])
    return x, st


def setup_inputs(seed: int = 0) -> dict:
    key = jax.random.key(seed)
    ks = iter(jax.random.split(key, 48))

    def nrm(shape, scale):
        return scale * jax.random.normal(next(ks), shape, jnp.float32)

    def gain(shape):
        return 1.0 + nrm(shape, 0.02)

    n_pages = PAST_LEN // PAGE_SIZE
    n_used = DEC_BATCH * n_pages
    n_pool = n_used + max(1, n_used // 4)
    x_prompt = nrm((BATCH, SEQ, D_MODEL), 1.0)
    x_sample = nrm((DEC_BATCH, DEC_SEQ, D_MODEL), 1.0)
    cache_k = nrm((DEPTH, n_pool, PAGE_SIZE, N_HEADS, HEAD_DIM), 1.0)
    cache_v = nrm((DEPTH, n_pool, PAGE_SIZE, N_HEADS, HEAD_DIM), 1.0)
    cache_logf = jax.nn.log_sigmoid(2.0 + nrm((DEPTH, n_pool, PAGE_SIZE, N_HEADS), 0.5))
    state_gla = nrm((DEPTH, DEC_BATCH, N_HEADS, GLA_DK, HEAD_DIM), 0.5)
    state_conv_c = nrm((DEPTH, DEC_BATCH, CONV_C_W - 1, GROUP_W), 1.0)
    state_rglru_h = nrm((DEPTH, DEC_BATCH, GROUP_W), 0.5)
    state_conv_d = nrm((DEPTH, DEC_BATCH, CONV_D_W - 1, GROUP_W), 1.0)
    page_table = jax.random.permutation(next(ks), n_pool)[:n_used].reshape(DEC_BATCH, n_pages).astype(jnp.int32)
    a_base = jax.random.uniform(next(ks), (DEPTH, GROUP_W), jnp.float32, 0.9, 0.999) ** (1.0 / LRU_C)
    lru_lambda = jnp.log(a_base) - jnp.log1p(-a_base)
    return {
        'x_prompt': x_prompt, 'x_sample': x_sample,
        'cache_k': cache_k, 'cache_v': cache_v, 'cache_logf': cache_logf,
        'state_gla': state_gla, 'state_conv_c': state_conv_c,
        'state_rglru_h': state_rglru_h, 'state_conv_d': state_conv_d,
        'page_table': page_table,
        'meta_tokens': nrm((N_META, D_MODEL), 1.0),
        'ln_ffn1': gain((DEPTH, D_MODEL)),
        'ffn1_gate': nrm((DEPTH, D_MODEL, D_FF), D_MODEL ** -0.5),
        'ffn1_up': nrm((DEPTH, D_MODEL, D_FF), D_MODEL ** -0.5),
        'ffn1_down': nrm((DEPTH, D_FF, D_MODEL), D_FF ** -0.5),
        'ln_mix': gain((DEPTH, D_MODEL)),
        'w_in': nrm((DEPTH, D_MODEL, D_IN), D_MODEL ** -0.5),
        'w_alpha_up': nrm((DEPTH, GLA_RANK, GLA_QK_W), GLA_RANK ** -0.5),
        'b_alpha': nrm((DEPTH, GLA_QK_W), 0.1),
        'b_forget': 2.0 + nrm((DEPTH, N_HEADS), 0.1),
        'conv_c_w': nrm((DEPTH, CONV_C_W, GROUP_W), CONV_C_W ** -0.5),
        'conv_d_w': nrm((DEPTH, CONV_D_W, GROUP_W), CONV_D_W ** -0.5),
        'conv_d_b': nrm((DEPTH, GROUP_W), 0.01),
        'lru_w_r': nrm((DEPTH, N_HEADS, HEAD_DIM, HEAD_DIM), HEAD_DIM ** -0.5),
        'lru_b_r': nrm((DEPTH, GROUP_W), 0.01),
        'lru_w_i': nrm((DEPTH, N_HEADS, HEAD_DIM, HEAD_DIM), HEAD_DIM ** -0.5),
        'lru_b_i': nrm((DEPTH, GROUP_W), 0.01),
        'lru_lambda': lru_lambda,
        'g_norm': gain((DEPTH, D_MODEL)),
        'w_out': nrm((DEPTH, D_MODEL, D_MODEL), D_MODEL ** -0.5),
        'ln_ffn2': gain((DEPTH, D_MODEL)),
        'ffn2_gate': nrm((DEPTH, D_MODEL, D_FF), D_MODEL ** -0.5),
        'ffn2_up': nrm((DEPTH, D_MODEL, D_FF), D_MODEL ** -0.5),
        'ffn2_down': nrm((DEPTH, D_FF, D_MODEL), D_FF ** -0.5),
        'ln_final': gain((D_MODEL,)),
    }


def reference(x_prompt, x_sample, cache_k, cache_v, cache_logf, state_gla, state_conv_c, state_rglru_h,
              state_conv_d, page_table, meta_tokens, ln_ffn1, ffn1_gate, ffn1_up, ffn1_down, ln_mix, w_in,
              w_alpha_up, b_alpha, b_forget, conv_c_w, conv_d_w, conv_d_b, lru_w_r, lru_b_r, lru_w_i, lru_b_i,
              lru_lambda, g_norm, w_out, ln_ffn2, ffn2_gate, ffn2_up, ffn2_down, ln_final):
    f32 = jnp.float32
    bp, seq = x_prompt.shape[0], x_prompt.shape[1]
    bs, dseq = x_sample.shape[0], x_sample.shape[1]
    n_pages = page_table.shape[1]
    xp = jnp.concatenate([jnp.broadcast_to(meta_tokens[None], (bp, N_META, D_MODEL)).astype(x_prompt.dtype),
                          x_prompt], axis=1)
    xs = x_sample
    seg_prompt = ((N_META, N_META), (seq, GLA_CHUNK))
    seg_sample = ((dseq, dseq),)
    gla0_p = jnp.zeros((bp, N_HEADS, GLA_DK, HEAD_DIM), f32)
    convc0_p = jnp.zeros((bp, CONV_C_W - 1, GROUP_W), f32)
    h0_p = jnp.zeros((bp, GROUP_W), f32)
    convd0_p = jnp.zeros((bp, CONV_D_W - 1, GROUP_W), f32)
    st_p, st_s = [], []
    for l in range(DEPTH):
        p = {'ln_ffn1': ln_ffn1[l], 'ffn1_gate': ffn1_gate[l], 'ffn1_up': ffn1_up[l], 'ffn1_down': ffn1_down[l],
             'ln_mix': ln_mix[l], 'w_in': w_in[l], 'w_alpha_up': w_alpha_up[l], 'b_alpha': b_alpha[l],
             'b_forget': b_forget[l], 'conv_c_w': conv_c_w[l], 'conv_d_w': conv_d_w[l], 'conv_d_b': conv_d_b[l],
             'lru_w_r': lru_w_r[l], 'lru_b_r': lru_b_r[l], 'lru_w_i': lru_w_i[l], 'lru_b_i': lru_b_i[l],
             'lru_lambda': lru_lambda[l], 'g_norm': g_norm[l], 'w_out': w_out[l],
             'ln_ffn2': ln_ffn2[l], 'ffn2_gate': ffn2_gate[l], 'ffn2_up': ffn2_up[l], 'ffn2_down': ffn2_down[l]}
        xp, sp = layer(xp, p, gla0_p, convc0_p, h0_p, convd0_p, seg_prompt, fox_prompt)
        k_past = cache_k[l][page_table].reshape(bs, n_pages * PAGE_SIZE, N_HEADS, HEAD_DIM)
        v_past = cache_v[l][page_table].reshape(bs, n_pages * PAGE_SIZE, N_HEADS, HEAD_DIM)
        lf_past = cache_logf[l][page_table].reshape(bs, n_pages * PAGE_SIZE, N_HEADS)
        fox_s = functools.partial(fox_sample, k_past=k_past, v_past=v_past, logf_past=lf_past)
        xs, ss = layer(xs, p, state_gla[l], state_conv_c[l], state_rglru_h[l], state_conv_d[l], seg_sample, fox_s)
        st_p.append(sp)
        st_s.append(ss)
    y_prompt = rmsnorm(xp, ln_final)[:, N_META:]
    y_sample = rmsnorm(xs, ln_final)
    k_prompt = jnp.stack([s[0] for s in st_p])
    v_prompt = jnp.stack([s[1] for s in st_p])
    logf_prompt = jnp.stack([s[2] for s in st_p])
    gla_prompt = jnp.stack([s[3] for s in st_p])
    conv_c_prompt = jnp.stack([s[4] for s in st_p])
    rglru_h_prompt = jnp.stack([s[5] for s in st_p])
    conv_d_prompt = jnp.stack([s[6] for s in st_p])
    k_sample = jnp.stack([s[0] for s in st_s])
    v_sample = jnp.stack([s[1] for s in st_s])
    logf_sample = jnp.stack([s[2] for s in st_s])
    gla_sample = jnp.stack([s[3] for s in st_s])
    conv_c_sample = jnp.stack([s[4] for s in st_s])
    rglru_h_sample = jnp.stack([s[5] for s in st_s])
    conv_d_sample = jnp.stack([s[6] for s in st_s])
    return (y_prompt, y_sample, k_prompt, v_prompt, logf_prompt, gla_prompt, conv_c_prompt, rglru_h_prompt,
            conv_d_prompt, k_sample, v_sample, logf_sample, gla_sample, conv_c_sample, rglru_h_sample,
            conv_d_sample)
```

```python
import numpy as np
import concourse.bass as bass
import concourse.mybir as mybir
from concourse.bass_utils import run_bass_kernel_spmd

F32 = mybir.dt.float32
BF16 = mybir.dt.bfloat16
I32 = mybir.dt.int32
AF = mybir.ActivationFunctionType
ALU = mybir.AluOpType
AX = mybir.AxisListType

D = 1024
DFF = 2816
DIN = 2836
EPS = 1e-6
C_QA, C_KA, C_VA, C_GA, C_RA = 0, 128, 256, 512, 768
C_QB, C_KB, C_VB, C_FB = 784, 1040, 1296, 1552
C_BC, C_CC, C_HC, C_GD, C_XD = 1556, 1812, 2068, 2324, 2580


class Prog:
    def __init__(self, nc, nslots=14):
        self.nc = nc
        self.ins = {e: [] for e in ("pe", "act", "dve", "pool", "sp")}
        self.cnt = {e: 0 for e in self.ins}
        self.lastw = {}
        self.rd = {}
        self.waited = {e: {} for e in self.ins}
        self.nslots = nslots
        self.slot_uses = {}
        self.slot_next = {}
        self.keys = set()
        self.bar = {e: {} for e in self.ins}

    def barrier(self):
        evs = {}
        for e in self.ins:
            if self.cnt[e] > 0:
                evs[("e", e)] = self.cnt[e]
        for k, u in self.slot_uses.items():
            evs[k] = 16 * u
        for e in self.ins:
            for k, v in evs.items():
                if self.bar[e].get(k, 0) < v:
                    self.bar[e][k] = v

    def op(self, eng, fn, r=(), w=(), dma=False):
        d = dict(self.bar[eng])
        self.bar[eng] = {}

        def add(k, v):
            if d.get(k, 0) < v:
                d[k] = v

        for t in r:
            if t in self.lastw:
                add(*self.lastw[t])
        for t in w:
            if t in self.lastw:
                add(*self.lastw[t])
            for k, v in self.rd.get(t, {}).items():
                add(k, v)
        if dma:
            slot = self.slot_next.get(eng, 0)
            self.slot_next[eng] = (slot + 1) % self.nslots
            key = ("d", eng, slot)
            uses = self.slot_uses.get(key, 0)
            if uses > 0:
                add(key, 16 * uses)
            self.slot_uses[key] = uses + 1
            ev = (key, 16 * (uses + 1))
            inc = 16
        else:
            self.cnt[eng] += 1
            key = ("e", eng)
            ev = (key, self.cnt[eng])
            inc = 1
        self.keys.add(key)
        waits = []
        wd = self.waited[eng]
        for k, v in d.items():
            if k == ("e", "pe") and eng == "pe":
                continue
            if wd.get(k, 0) >= v:
                continue
            wd[k] = v
            waits.append((k, v))
        self.ins[eng].append((waits, fn, key, inc))
        for t in w:
            self.lastw[t] = ev
            self.rd[t] = {}
        for t in r:
            rr = self.rd.setdefault(t, {})
            if rr.get(ev[0], 0) < ev[1]:
                rr[ev[0]] = ev[1]
        return ev

    def dma(self, out, in_, r=(), w=(), q="sp", **kw):
        return self.op(q, lambda e: e.dma_start(out=out, in_=in_, **kw), r, w, dma=True)

    def mm(self, out, lhsT, rhs, start=True, stop=True, r=(), w=()):
        return self.op("pe", lambda e: e.matmul(out, lhsT, rhs, start=start, stop=stop), r, w)

    def tr(self, out, in_, ident, r=(), w=()):
        return self.op("pe", lambda e: e.transpose(out, in_, ident), r, w)

    def act(self, out, in_, func, r=(), w=(), eng="act", **kw):
        return self.op(eng, lambda e: e.activation(out=out, in_=in_, func=func, **kw), r, w)

    def tt(self, eng, out, in0, in1, op, r=(), w=()):
        return self.op(eng, lambda e: e.tensor_tensor(out=out, in0=in0, in1=in1, op=op), r, w)

    def ts(self, eng, out, in0, s1, s2, op0, op1=None, r=(), w=()):
        if op1 is None:
            return self.op(eng, lambda e: e.tensor_scalar(out=out, in0=in0, scalar1=s1, scalar2=None, op0=op0), r, w)
        return self.op(eng, lambda e: e.tensor_scalar(out=out, in0=in0, scalar1=s1, scalar2=s2, op0=op0, op1=op1), r, w)

    def stt(self, eng, out, in0, scalar, in1, op0, op1, r=(), w=()):
        return self.op(eng, lambda e: e.scalar_tensor_tensor(out=out, in0=in0, scalar=scalar, in1=in1, op0=op0, op1=op1), r, w)

    def copy(self, eng, out, in_, r=(), w=()):
        if eng == "act":
            return self.op(eng, lambda e: e.copy(out=out, in_=in_), r, w)
        return self.op(eng, lambda e: e.tensor_copy(out=out, in_=in_), r, w)

    def memset(self, eng, ap, val, w=()):
        return self.op(eng, lambda e: e.memset(ap, val), (), w)

    def scan(self, eng, out, d0, d1, init, r=(), w=()):
        return self.op(eng, lambda e: e.tensor_tensor_scan(out=out, data0=d0, data1=d1, initial=init,
                                                           op0=ALU.mult, op1=ALU.add), r, w)

    def emit(self, stack):
        nc = self.nc
        sems = {}
        for i, k in enumerate(sorted(self.keys, key=str)):
            sems[k] = stack.enter_context(nc.semaphore("s%d" % i))
        finals = {}
        for k in self.keys:
            if k[0] == "e":
                finals[k] = self.cnt[k[1]]
            else:
                finals[k] = 16 * self.slot_uses[k]
        block = stack.enter_context(nc.Block())

        def mk(name, last=False):
            def f(eng):
                for waits, fn, key, inc in self.ins[name]:
                    for k, v in waits:
                        eng.wait_ge(sems[k], v)
                    fn(eng).then_inc(sems[key], inc)
                if last:
                    for k, v in finals.items():
                        if v > 0:
                            eng.wait_ge(sems[k], v)
            return f

        block.tensor(mk("pe"))
        block.scalar(mk("act"))
        block.vector(mk("dve"))
        block.gpsimd(mk("pool"))
        block.sync(mk("sp", last=True))


def build(cfg):
    from contextlib import ExitStack
    L = cfg["L"]
    NS = cfg["NS"]
    NT = NS * 4
    LT = L + NT
    DEPTH = cfg["depth"]
    NPOOL = cfg["n_pool"]
    NPG = cfg["n_pages"]
    stages = cfg.get("stages", "all")
    nc = bass.Bass("TRN2", target_bir_lowering=False)

    def din(name, shape, dt=F32):
        return nc.dram_tensor(name, list(shape), dt, kind="ExternalInput").ap()

    def dout(name, shape, dt=F32):
        return nc.dram_tensor(name, list(shape), dt, kind="ExternalOutput").ap()

    def dscr(name, shape, dt=F32):
        return nc.dram_tensor(name, list(shape), dt, kind="Internal").ap()

    I = {}
    I["x_prompt"] = din("x_prompt", [L - 16, D])
    I["x_sample"] = din("x_sample", [NT, D])
    I["meta_tokens"] = din("meta_tokens", [16, D])
    I["cache_k"] = din("cache_k", [DEPTH, NPOOL * 128, 256])
    I["cache_v"] = din("cache_v", [DEPTH, NPOOL * 128, 256])
    I["cache_logf"] = din("cache_logf", [DEPTH, NPOOL, 512])
    I["state_gla"] = din("state_gla", [DEPTH, NS, 128, 64])
    I["state_conv_c"] = din("state_conv_c", [DEPTH, NS, 2, 256])
    I["state_rglru_h"] = din("state_rglru_h", [DEPTH, NS, 256])
    I["state_conv_d"] = din("state_conv_d", [DEPTH, NS, 3, 256])
    I["page_table"] = din("page_table", [1, NS * NPG], I32)
    for nm in ("ln_ffn1", "ln_mix", "g_norm", "ln_ffn2"):
        I[nm] = din(nm, [DEPTH, D])
    for nm in ("ffn1_gate", "ffn1_up", "ffn2_gate", "ffn2_up"):
        I[nm] = din(nm, [DEPTH, D, DFF])
    for nm in ("ffn1_down", "ffn2_down"):
        I[nm] = din(nm, [DEPTH, DFF, D])
    I["w_in"] = din("w_in", [DEPTH, D, DIN])
    I["w_out"] = din("w_out", [DEPTH, D, D])
    I["w_alpha_up"] = din("w_alpha_up", [DEPTH, 16, 128])
    I["b_alpha"] = din("b_alpha", [DEPTH, 128])
    I["b_forget"] = din("b_forget", [DEPTH, 4])
    I["conv_c_w"] = din("conv_c_w", [DEPTH, 3, 256])
    I["conv_d_w"] = din("conv_d_w", [DEPTH, 4, 256])
    for nm in ("conv_d_b", "lru_b_r", "lru_b_i", "lru_lambda"):
        I[nm] = din(nm, [DEPTH, 256])
    I["lru_w_r"] = din("lru_w_r", [DEPTH, 4, 64, 64])
    I["lru_w_i"] = din("lru_w_i", [DEPTH, 4, 64, 64])
    I["ln_final"] = din("ln_final", [1, D])

    O = {}
    O["y_prompt"] = dout("y_prompt", [L - 16, D])
    O["y_sample"] = dout("y_sample", [NT, D])
    O["k_prompt"] = dout("k_prompt", [DEPTH, L, 256])
    O["v_prompt"] = dout("v_prompt", [DEPTH, L, 256])
    O["logf_prompt"] = dout("logf_prompt", [DEPTH, L, 4])
    O["gla_prompt"] = dout("gla_prompt", [DEPTH, 128, 64])
    O["conv_c_prompt"] = dout("conv_c_prompt", [DEPTH, 2, 256])
    O["rglru_h_prompt"] = dout("rglru_h_prompt", [DEPTH, 1, 256])
    O["conv_d_prompt"] = dout("conv_d_prompt", [DEPTH, 3, 256])
    O["k_sample"] = dout("k_sample", [DEPTH, NT, 256])
    O["v_sample"] = dout("v_sample", [DEPTH, NT, 256])
    O["logf_sample"] = dout("logf_sample", [DEPTH, NT, 4])
    O["gla_sample"] = dout("gla_sample", [DEPTH, NS, 128, 64])
    O["conv_c_sample"] = dout("conv_c_sample", [DEPTH, NS, 2, 256])
    O["rglru_h_sample"] = dout("rglru_h_sample", [DEPTH, NS, 256])
    O["conv_d_sample"] = dout("conv_d_sample", [DEPTH, NS, 3, 256])

    X = dscr("X", [LT, D])
    Z = dscr("Z", [LT, DIN])
    ZT = dscr("ZT", [DIN, LT])
    DBG = bool(cfg.get("dbg"))
    if DBG:
        O["dbg_pp"] = dout("dbg_pp", [16, 16])
        O["dbg_kt"] = dout("dbg_kt", [128, 16], BF16)
        O["dbg_qt"] = dout("dbg_qt", [128, 16], BF16)
    YT = (dout if cfg.get("dbg") else dscr)("YT", [D, LT])

    P = Prog(nc)
    st = ExitStack()

    def sb(name, shape, dt=F32):
        return st.enter_context(nc.sbuf_tensor(name, list(shape), dt))

    def ps(name, shape, dt=F32):
        return st.enter_context(nc.psum_tensor(name, list(shape), dt))

    ident_f = sb("ident_f", [128, 128])
    ident_b = sb("ident_b", [128, 128], BF16)
    ones_f = sb("ones_f", [128, 128])
    P.memset("pool", ones_f[:], 1.0, w=["ones_f"])
    P.memset("pool", ident_f[:], 0.0, w=["ident_f"])
    P.op("pool", lambda e: e.affine_select(out=ident_f[:], in_=ones_f[:], pattern=[[1, 128]],
                                           compare_op=ALU.is_equal, fill=0.0, base=0, channel_multiplier=-1),
         r=["ones_f"], w=["ident_f"])
    P.copy("dve", ident_b[:], ident_f[:], r=["ident_f"], w=["ident_b"])
    epsc = sb("epsc", [128, 1])
    P.memset("pool", epsc[:], EPS, w=["epsc"])

    groups = []
    r0 = 0
    while r0 < LT:
        n = min(512, LT - r0)
        subs = []
        o = 0
        while o < n:
            subs.append((o, min(128, n - o)))
            o += 128
        groups.append((r0, n, subs))
        r0 += n

    ARENA = 204800
    arena = sb("arena", [128, ARENA // 2], BF16)

    class Carver:
        def __init__(self, off=0):
            self.off = off

        def get(self, shape, dt=F32):
            esz = 4 if dt in (F32, I32) else 2
            n = 1
            for x in shape[1:]:
                n *= x
            nb = (n * esz + 31) // 32 * 32
            assert self.off + nb <= ARENA, ("arena overflow", self.off, nb)
            ap = arena[:, self.off // 2:(self.off + n * esz) // 2]
            self.off += nb
            if dt != BF16:
                ap = ap.bitcast(dt)
            ap = ap[0:shape[0], :]
            if len(shape) == 3:
                ap = ap.rearrange("p (a b) -> p a b", a=shape[1])
            elif len(shape) == 4:
                ap = ap.rearrange("p (a b c) -> p a b c", a=shape[1], b=shape[2])
            return ap

    cv = Carver()
    Wg = cv.get([128, 8, DFF], BF16)
    Wu = cv.get([128, 8, DFF], BF16)
    Wd = cv.get([128, 22, D], BF16)
    wtmp = Carver(0)
    Win = wtmp.get([128, 8, DIN], BF16)
    wtmp = Carver(0)
    Wout = wtmp.get([128, 8, D], BF16)

    xs = cv.get([128, 4, D])
    gvec = cv.get([128, D])
    hb = [cv.get([128, D], BF16) for i in range(2)]
    hT = cv.get([128, 8, 512], BF16)
    actT = cv.get([128, 22, 512], BF16)
    sg = [cv.get([128, 512]) for i in range(2)]
    junk = cv.get([128, D], BF16)
    stage = cv.get([128, 4, 512])
    ssq = sb("ssq", [128, 8])
    rstd = sb("rstd", [128, 8])

    pT = ps("pT", [128, 8, 128], BF16)
    pA = [ps("pA%d" % i, [128, 512]) for i in range(2)]
    pB = [ps("pB%d" % i, [128, 512]) for i in range(2)]
    pO = [ps("pO%d" % i, [128, 512]) for i in range(2)]

    def load_w_cast(dst3, src2, nk, width, tok):
        for k in range(nk):
            c = 0
            while c < width:
                cw = min(1024, width - c)
                P.dma(dst3[:, k, c:c + cw], src2[k * 128:(k + 1) * 128, c:c + cw], w=[tok], q="pool")
                c += cw

    def load_group_x(r0, subs):
        for j, (o, rows) in enumerate(subs):
            P.dma(xs[:rows, j, :], X[r0 + o:r0 + o + rows, :], r=[("X", r0 + o)], w=[("xs", j)])

    def store_group_x(r0, subs):
        for j, (o, rows) in enumerate(subs):
            P.dma(X[r0 + o:r0 + o + rows, :], xs[:rows, j, :], r=[("xs", j)], w=[("X", r0 + o)], q="sp")

    def rstd_of_xs(subs):
        ns = len(subs)
        P.memset("dve", ssq[:, 0:ns], 0.0, w=["ssq"])
        for j, (o, rows) in enumerate(subs):
            P.act(junk[:rows, :], xs[:rows, j, :], AF.Square, r=[("xs", j), "ssq"], w=["junk", "ssq"],
                  accum_out=ssq[:rows, j:j + 1])
        P.act(rstd[:, 0:ns], ssq[:, 0:ns], AF.Sqrt, r=["ssq"], w=["rstd"], bias=epsc[:, 0:1], scale=1.0 / D)
        P.op("dve", lambda e: e.reciprocal(out=rstd[:, 0:ns], in_=rstd[:, 0:ns]), r=["rstd"], w=["rstd"])

    def norm_to_hT(subs, gain=True):
        rstd_of_xs(subs)
        for j, (o, rows) in enumerate(subs):
            h = hb[j % 2]
            P.stt("dve", h[:rows, :], xs[:rows, j, :], rstd[:rows, j:j + 1], gvec[:rows, :], ALU.mult, ALU.mult,
                  r=[("xs", j), "rstd", "gvec"], w=[("hb", j % 2)])
            for k in range(8):
                P.tr(pT[:, k, :rows], h[:rows, k * 128:(k + 1) * 128], ident_b[:rows, :rows],
                     r=[("hb", j % 2), "ident_b"], w=["pT"])
            P.copy("act", hT[:, :, o:o + rows], pT[:, :, :rows], r=["pT"], w=["hT"])

    def pass_init():
        P.dma(X[0:16, :], I["meta_tokens"][:, :], w=[("X", 0)])
        P.dma(X[16:L, :], I["x_prompt"][:, :], w=[("X", g[0] + o) for g in groups for (o, _) in g[2]])
        P.dma(X[L:LT, :], I["x_sample"][:, :], w=[("X", g[0] + o) for g in groups for (o, _) in g[2]])

    def pass_ffn(l, which):
        ln = I["ln_ffn%d" % which][l:l + 1, :]
        P.dma(gvec[:, :], ln.partition_broadcast(128) if False else ln.to_broadcast([128, D]), w=["gvec"])
        load_w_cast(Wg, I["ffn%d_gate" % which][l], 8, DFF, "Wg")
        load_w_cast(Wu, I["ffn%d_up" % which][l], 8, DFF, "Wu")
        load_w_cast(Wd, I["ffn%d_down" % which][l], 22, D, "Wd")
        for (r0, n, subs) in groups:
            load_group_x(r0, subs)
            norm_to_hT(subs)
            for f in range(22):
                a = pA[f % 2]
                b = pB[f % 2]
                for k in range(8):
                    P.mm(a[:, :n], Wg[:, k, f * 128:(f + 1) * 128], hT[:, k, :n], start=(k == 0), stop=(k == 7),
                         r=["Wg", "hT"], w=[("pA", f % 2)])
                for k in range(8):
                    P.mm(b[:, :n], Wu[:, k, f * 128:(f + 1) * 128], hT[:, k, :n], start=(k == 0), stop=(k == 7),
                         r=["Wu", "hT"], w=[("pB", f % 2)])
                s_ = sg[f % 2]
                P.act(s_[:, :n], a[:, :n], AF.Silu, r=[("pA", f % 2)], w=[("sg", f % 2)])
                P.tt("dve", actT[:, f, :n], s_[:, :n], b[:, :n], ALU.mult,
                     r=[("sg", f % 2), ("pB", f % 2)], w=[("actT", f)])
            cnt = 0
            for j, (o, rows) in enumerate(subs):
                for dh in range(2):
                    po = pO[cnt % 2]
                    for f in range(22):
                        P.mm(po[:rows, :], actT[:, f, o:o + rows], Wd[:, f, dh * 512:(dh + 1) * 512],
                             start=(f == 0), stop=(f == 21), r=[("actT", f), "Wd"], w=[("pO", cnt % 2)])
                    P.stt("dve", xs[:rows, j, dh * 512:(dh + 1) * 512], po[:rows, :], 0.5,
                          xs[:rows, j, dh * 512:(dh + 1) * 512], ALU.mult, ALU.add,
                          r=[("pO", cnt % 2), ("xs", j)], w=[("xs", j)])
                    cnt += 1
            store_group_x(r0, subs)

    T_CHUNKS = [(C_QA, 128), (C_KA, 128), (C_RA, 16), (C_QB, 128), (C_QB + 128, 128), (C_KB, 128), (C_KB + 128, 128),
                (C_FB, 4), (C_BC, 128), (C_BC + 128, 128), (C_CC, 128), (C_CC + 128, 128), (C_HC, 128),
                (C_HC + 128, 128), (C_GD, 128), (C_GD + 128, 128), (C_XD, 128), (C_XD + 128, 128)]
    R_CHUNKS = [(C_VA, 512), (C_KB, 512), (C_FB, 4)]

    def pass_in(l):
        ln = I["ln_mix"][l:l + 1, :]
        P.dma(gvec[:, :], ln.to_broadcast([128, D]), w=["gvec"])
        load_w_cast(Win, I["w_in"][l], 8, DIN, "Wg")
        P.op("pool", lambda e: e.memset(junk[0:1, 0:1], 0.0), r=[], w=["Wu", "Wd", "junk"])
        for (r0, n, subs) in groups:
            load_group_x(r0, subs)
            norm_to_hT(subs)
            ci = 0
            for (c0, cw) in T_CHUNKS:
                a = pA[ci % 2]
                for k in range(8):
                    P.mm(a[:cw, :n], Win[:, k, c0:c0 + cw], hT[:, k, :n], start=(k == 0), stop=(k == 7),
                         r=["Wg", "Wu", "Wd", "hT"], w=[("pA", ci % 2)])
                s_ = stage[:, ci % 4, :]
                P.copy("act" if ci % 2 == 0 else "dve", s_[:cw, :n], a[:cw, :n], r=[("pA", ci % 2)], w=[("stage", ci % 4)])
                P.dma(ZT[c0:c0 + cw, r0:r0 + n], s_[:cw, :n], r=[("stage", ci % 4)], w=[("ZT", r0)], q="sp")
                ci += 1
            for j, (o, rows) in enumerate(subs):
                for (c0, cw) in R_CHUNKS:
                    a = pA[ci % 2]
                    for k in range(8):
                        P.mm(a[:rows, :cw], hT[:, k, o:o + rows], Win[:, k, c0:c0 + cw], start=(k == 0), stop=(k == 7),
                             r=["Wg", "Wu", "Wd", "hT"], w=[("pA", ci % 2)])
                    s_ = stage[:, ci % 4, :]
                    P.copy("act" if ci % 2 == 0 else "dve", s_[:rows, :cw], a[:rows, :cw], r=[("pA", ci % 2)],
                           w=[("stage", ci % 4)])
                    P.dma(Z[r0 + o:r0 + o + rows, c0:c0 + cw], s_[:rows, :cw], r=[("stage", ci % 4)],
                          w=[("Z", r0)], q="sp")
                    ci += 1

    yts = stage
    ytb = hT
    gn = sb("gn", [128, 8])

    def pass_out(l):
        load_w_cast(Wout, I["w_out"][l], 8, D, "Wg")
        P.op("pool", lambda e: e.memset(junk[0:1, 0:1], 0.0), r=[], w=["Wu", "Wd", "junk"])
        P.dma(gn[:, :], I["g_norm"][l].rearrange("(k p) -> p k", p=128), w=["gn"], allow_slow_non_contiguous=True)
        for (r0, n, subs) in groups:
            load_group_x(r0, subs)
            for k in range(8):
                s_ = stage[:, k % 4, :]
                P.dma(s_[:, :n], YT[k * 128:(k + 1) * 128, r0:r0 + n], r=[("YT", r0)], w=[("stage", k % 4)])
                P.ts("dve" if k % 2 else "pool", ytb[:, k, :n], s_[:, :n], gn[:, k:k + 1], None, ALU.mult,
                     r=[("stage", k % 4), "gn"], w=["hT"])
            cnt = 0
            for j, (o, rows) in enumerate(subs):
                for dh in range(2):
                    po = pO[cnt % 2]
                    for k in range(8):
                        P.mm(po[:rows, :], ytb[:, k, o:o + rows], Wout[:, k, dh * 512:(dh + 1) * 512],
                             start=(k == 0), stop=(k == 7), r=["hT", "Wg", "Wu", "Wd"], w=[("pO", cnt % 2)])
                    P.tt("dve", xs[:rows, j, dh * 512:(dh + 1) * 512], po[:rows, :],
                         xs[:rows, j, dh * 512:(dh + 1) * 512], ALU.add,
                         r=[("pO", cnt % 2), ("xs", j)], w=[("xs", j)])
                    cnt += 1
            store_group_x(r0, subs)

    def pass_final():
        P.dma(gvec[:, :], I["ln_final"][0:1, :].to_broadcast([128, D]), w=["gvec"])
        for (r0, n, subs) in groups:
            load_group_x(r0, subs)
            rstd_of_xs(subs)
            for j, (o, rows) in enumerate(subs):
                P.stt("dve", xs[:rows, j, :], xs[:rows, j, :], rstd[:rows, j:j + 1], gvec[:rows, :], ALU.mult, ALU.mult,
                      r=[("xs", j), "rstd", "gvec"], w=[("xs", j)])
                a = r0 + o
                b = a + rows
                lo, hi = max(a, 16), min(b, L)
                if lo < hi:
                    P.dma(O["y_prompt"][lo - 16:hi - 16, :], xs[lo - a:hi - a, j, :], r=[("xs", j)], w=["y_prompt"], q="sp")
                lo, hi = max(a, L), min(b, LT)
                if lo < hi:
                    P.dma(O["y_sample"][lo - L:hi - L, :], xs[lo - a:hi - a, j, :], r=[("xs", j)], w=["y_sample"], q="sp")


    mask_qk = sb("mask_qk", [128, 128])
    tri_le = sb("tri_le", [128, 128])
    tri_gt = sb("tri_gt", [128, 128])
    onec = sb("onec", [128, 1])
    iota_f = sb("iota_f", [128, 1])
    iota_i = sb("iota_i", [128, 1], I32)
    P.memset("pool", onec[:], 1.0, w=["onec"])
    P.op("pool", lambda e: e.affine_select(out=mask_qk[:], in_=ones_f[:], pattern=[[1, 128]], compare_op=ALU.is_ge,
                                           fill=0.0, base=0, channel_multiplier=-1), r=["ones_f"], w=["mask_qk"])
    P.op("pool", lambda e: e.affine_select(out=tri_le[:], in_=ones_f[:], pattern=[[1, 128]], compare_op=ALU.is_ge,
                                           fill=0.0, base=0, channel_multiplier=-1), r=["ones_f"], w=["tri_le"])
    P.op("pool", lambda e: e.affine_select(out=tri_gt[:], in_=ones_f[:], pattern=[[-1, 128]], compare_op=ALU.is_gt,
                                           fill=0.0, base=0, channel_multiplier=1), r=["ones_f"], w=["tri_gt"])
    P.op("pool", lambda e: e.iota(iota_i[:], pattern=[[0, 1]], base=0, channel_multiplier=1), w=["iota_i"])
    P.copy("dve", iota_f[:], iota_i[:], r=["iota_i"], w=["iota_f"])

    hm2 = sb("hm2", [128, 2])
    P.memset("dve", hm2[:], 0.0, w=["hm2"])
    P.memset("dve", hm2[0:64, 0:1], 0.125, w=["hm2"])
    P.memset("dve", hm2[64:128, 1:2], 0.125, w=["hm2"])
    NTILE = (L + 127) // 128
    def tile_rows(t):
        return min(128, L - t * 128)

    def pass_mix(l):
        P.barrier()
        m = Carver(0)
        cw = m.get([128, 2, 3]); dw = m.get([128, 2, 4]); db = m.get([128, 2]); br = m.get([128, 2]); bi = m.get([128, 2])
        lam = m.get([128, 2]); coef = m.get([128, 2]); coef2 = m.get([128, 2])
        nba = m.get([128, 1]); bfr = m.get([128, 4]); wal = m.get([16, 128])
        BDr = [m.get([128, 128], BF16) for _ in range(2)]; BDi = [m.get([128, 128], BF16) for _ in range(2)]
        kw = dict(allow_slow_non_contiguous=True)
        for c in range(2):
            P.dma(cw[:, c, :], I["conv_c_w"][l][:, c * 128:(c + 1) * 128].rearrange("j p -> p j"), w=["prm"], **kw)
            P.dma(dw[:, c, :], I["conv_d_w"][l][:, c * 128:(c + 1) * 128].rearrange("j p -> p j"), w=["prm"], **kw)
        for t_, nm in ((db, "conv_d_b"), (br, "lru_b_r"), (bi, "lru_b_i"), (lam, "lru_lambda")):
            P.dma(t_, I[nm][l].rearrange("(c p) -> p c", p=128), w=["prm"], **kw)
        P.dma(nba, I["b_alpha"][l].rearrange("(p o) -> p o", o=1), w=["prm"], **kw)
        P.dma(bfr, I["b_forget"][l:l + 1, :].to_broadcast([128, 4]), w=["prm"])
        P.dma(wal, I["w_alpha_up"][l], w=["prm"])
        for c in range(2):
            for (BD, nm) in ((BDr, "lru_w_r"), (BDi, "lru_w_i")):
                P.memset("pool", BD[c], 0.0, w=[("BD", nm, c)])
                for hh in range(2):
                    P.dma(BD[c][hh * 64:(hh + 1) * 64, hh * 64:(hh + 1) * 64], I[nm][l, 2 * c + hh], w=[("BD", nm, c)], q="pool")
        P.act(coef, lam, AF.Exp, r=["prm"], w=["coef"], scale=-1.0)
        P.act(coef, coef, AF.Ln, r=["coef"], w=["coef"], bias=onec[:, 0:1])
        P.ts("dve", coef2, coef, -16.0, None, ALU.mult, r=["coef"], w=["coef2"])
        P.ts("dve", coef, coef, -8.0, None, ALU.mult, r=["coef", "coef2"], w=["coef"])
        P.ts("dve", nba, nba, -1.0, None, ALU.mult, r=["prm"], w=["nba"])

        prm_end = m.off
        NW = 520
        tl = {nm: m.get([128, 2, NW]) for nm in ("bc", "ta", "tb", "U", "acc", "gd", "XD", "xc", "R", "IG", "A", "UU", "H", "t1")}
        xcb = m.get([128, 2, 512], BF16)
        rs = m.get([128, 512])
        hcar = m.get([128, 2]); h0s = m.get([128, 2, max(NS, 1)])
        P.memset("dve", hcar, 0.0, w=["hcar"])
        psr = pA[0]; psi = pA[1]; pss = pB[0]

        def v3(t, c, nseq, w, a, b):
            return t[:, c, 0:nseq * w].rearrange("p (s w) -> p s w", s=nseq)[:, :, a:b]

        def rms_T_store(V, vtok, n, sq, sqtok):
            for c in range(2):
                P.tt("pool", sq[:, c, :n], V[:, c, :n], V[:, c, :n], ALU.mult, r=[vtok], w=[sqtok])
            for c in range(2):
                P.mm(pss[:, :n], ones_f[:, :], sq[:, c, :n], start=(c == 0), stop=(c == 1), r=[sqtok, "ones_f"], w=["pB0"])
            P.act(rs[:, :n], pss[:, :n], AF.Sqrt, r=["pB0"], w=["rs"], bias=epsc[:, 0:1], scale=1.0 / 256)
            P.op("dve", lambda e: e.reciprocal(out=rs[:, :n], in_=rs[:, :n]), r=["rs"], w=["rs"])
            for c in range(2):
                P.tt("dve", sq[:, c, :n], V[:, c, :n], rs[:, :n], ALU.mult, r=[vtok, "rs", sqtok], w=[sqtok])
            return sq

        def mix_cd(c0, nseq, T, mode, first, last):
            n = nseq * T
            U, XD = tl["U"], tl["XD"]
            if mode == "p":
                if first:
                    for c in range(2):
                        P.memset("pool", U[:, c, 0:2], 0.0, w=["U"])
                        P.memset("pool", XD[:, c, 0:3], 0.0, w=["XD"])
            else:
                for c in range(2):
                    for s_ in range(nseq):
                        P.dma(v3(U, c, nseq, 2 + T, 0, 2)[:, s_, :], I["state_conv_c"][l][s_, :, c * 128:(c + 1) * 128].rearrange("j p -> p j"),
                              w=["U"], **kw)
                        P.dma(v3(XD, c, nseq, 3 + T, 0, 3)[:, s_, :], I["state_conv_d"][l][s_, :, c * 128:(c + 1) * 128].rearrange("j p -> p j"),
                              w=["XD"], **kw)
                    P.dma(h0s[:, c, 0:nseq], I["state_rglru_h"][l][:, c * 128:(c + 1) * 128].rearrange("s p -> p s"), w=["h0s"], **kw)
            for c in range(2):
                P.dma(tl["bc"][:, c, :n], ZT[C_BC + c * 128:C_BC + (c + 1) * 128, c0:c0 + n], w=["bc"])
                P.dma(tl["ta"][:, c, :n], ZT[C_CC + c * 128:C_CC + (c + 1) * 128, c0:c0 + n], w=["ta"])
                P.dma(tl["tb"][:, c, :n], ZT[C_HC + c * 128:C_HC + (c + 1) * 128, c0:c0 + n], w=["tb"])
                P.dma(tl["gd"][:, c, :n], ZT[C_GD + c * 128:C_GD + (c + 1) * 128, c0:c0 + n], w=["gd"])
                P.dma(v3(XD, c, nseq, 3 + T, 3, 3 + T),
                      ZT[C_XD + c * 128:C_XD + (c + 1) * 128, c0:c0 + n].rearrange("p (s t) -> p s t", s=nseq), r=["XD"], w=["XD"])
            acc = tl["acc"]
            for c in range(2):
                P.tt("pool", v3(U, c, nseq, 2 + T, 2, 2 + T), tl["ta"][:, c, :n].rearrange("p (s t) -> p s t", s=nseq),
                     tl["tb"][:, c, :n].rearrange("p (s t) -> p s t", s=nseq), ALU.mult, r=["ta", "tb", "U"], w=["U"])
                a3 = acc[:, c, :n].rearrange("p (s t) -> p s t", s=nseq)
                P.ts("dve", a3, v3(U, c, nseq, 2 + T, 0, T), cw[:, c, 0:1], None, ALU.mult, r=["U", "prm"], w=["acc"])
                P.stt("dve", a3, v3(U, c, nseq, 2 + T, 1, 1 + T), cw[:, c, 1:2], a3, ALU.mult, ALU.add, r=["U", "acc"], w=["acc"])
                P.stt("dve", a3, v3(U, c, nseq, 2 + T, 2, 2 + T), cw[:, c, 2:3], a3, ALU.mult, ALU.add, r=["U", "acc"], w=["acc"])
                P.tt("dve", acc[:, c, :n], acc[:, c, :n], tl["bc"][:, c, :n], ALU.mult, r=["acc", "bc"], w=["acc"])
            y = rms_T_store(acc, "acc", n, tl["t1"], "t1")
            for c in range(2):
                P.dma(YT[512 + c * 128:512 + (c + 1) * 128, c0:c0 + n], y[:, c, :n], r=["t1"], w=["YTc"])
            if mode == "p":
                if last:
                    for c in range(2):
                        P.dma(O["conv_c_prompt"][l][:, c * 128:(c + 1) * 128].rearrange("j p -> p j"), U[:, c, T:T + 2],
                              r=["U"], w=["o_cc"], **kw)
                else:
                    for c in range(2):
                        P.copy("pool", tl["ta"][:, c, 0:2], U[:, c, T:T + 2], r=["U", "ta"], w=["ta"])
                        P.copy("pool", U[:, c, 0:2], tl["ta"][:, c, 0:2], r=["ta", "U"], w=["U"])
            else:
                for c in range(2):
                    for s_ in range(nseq):
                        P.dma(O["conv_c_sample"][l][s_, :, c * 128:(c + 1) * 128].rearrange("j p -> p j"),
                              v3(U, c, nseq, 2 + T, T, T + 2)[:, s_, :], r=["U"], w=["o_cc"], **kw)
            xc = tl["xc"]
            for c in range(2):
                x3 = xc[:, c, :n].rearrange("p (s t) -> p s t", s=nseq)
                P.ts("dve", x3, v3(XD, c, nseq, 3 + T, 0, T), dw[:, c, 0:1], db[:, c:c + 1], ALU.mult, ALU.add, r=["XD", "prm"], w=["xc"])
                for j in range(1, 4):
                    P.stt("dve", x3, v3(XD, c, nseq, 3 + T, j, j + T), dw[:, c, j:j + 1], x3, ALU.mult, ALU.add, r=["XD", "xc"], w=["xc"])
                P.copy("pool", xcb[:, c, :n], xc[:, c, :n], r=["xc"], w=["xcb"])
            if mode == "p":
                if last:
                    for c in range(2):
                        P.dma(O["conv_d_prompt"][l][:, c * 128:(c + 1) * 128].rearrange("j p -> p j"), XD[:, c, T:T + 3],
                              r=["XD"], w=["o_cd"], **kw)
                else:
                    for c in range(2):
                        P.copy("pool", tl["tb"][:, c, 0:3], XD[:, c, T:T + 3], r=["XD", "tb"], w=["tb"])
                        P.copy("pool", XD[:, c, 0:3], tl["tb"][:, c, 0:3], r=["tb", "XD"], w=["XD"])
            else:
                for c in range(2):
                    for s_ in range(nseq):
                        P.dma(O["conv_d_sample"][l][s_, :, c * 128:(c + 1) * 128].rearrange("j p -> p j"),
                              v3(XD, c, nseq, 3 + T, T, T + 3)[:, s_, :], r=["XD"], w=["o_cd"], **kw)
            R, IG, A_, UU, H, t1 = tl["R"], tl["IG"], tl["A"], tl["UU"], tl["H"], tl["t1"]
            for c in range(2):
                P.mm(psr[:, :n], BDr[c], xcb[:, c, :n], r=[("BD", "lru_w_r", c), "xcb"], w=["pA0"])
                P.mm(psi[:, :n], BDi[c], xcb[:, c, :n], r=[("BD", "lru_w_i", c), "xcb"], w=["pA1"])
                P.act(R[:, c, :n], psr[:, :n], AF.Sigmoid, r=["pA0", "prm"], w=["R"], bias=br[:, c:c + 1])
                P.act(IG[:, c, :n], psi[:, :n], AF.Sigmoid, r=["pA1", "prm"], w=["IG"], bias=bi[:, c:c + 1])
            for c in range(2):
                P.act(A_[:, c, :n], R[:, c, :n], AF.Exp, r=["R", "coef"], w=["A"], scale=coef[:, c:c + 1])
                P.act(t1[:, c, :n], R[:, c, :n], AF.Exp, r=["R", "coef2"], w=["t1"], scale=coef2[:, c:c + 1])
                P.ts("dve", t1[:, c, :n], t1[:, c, :n], -1.0, 1.0, ALU.mult, ALU.add, r=["t1"], w=["t1"])
            for c in range(2):
                P.act(t1[:, c, :n], t1[:, c, :n], AF.Sqrt, r=["t1"], w=["t1"])
                P.tt("dve", UU[:, c, :n], IG[:, c, :n], xc[:, c, :n], ALU.mult, r=["IG", "xc"], w=["UU"])
                P.tt("dve", UU[:, c, :n], UU[:, c, :n], t1[:, c, :n], ALU.mult, r=["UU", "t1"], w=["UU"])
            for c in range(2):
                if mode == "p":
                    P.scan("dve", H[:, c, :n], A_[:, c, :n], UU[:, c, :n], hcar[:, c:c + 1], r=["A", "UU", "hcar"], w=["H"])
                    P.copy("dve", hcar[:, c:c + 1], H[:, c, n - 1:n], r=["H", "hcar"], w=["hcar"])
                else:
                    a3 = A_[:, c, :n].rearrange("p (s t) -> p s t", s=nseq)
                    u3 = UU[:, c, :n].rearrange("p (s t) -> p s t", s=nseq)
                    P.tt("dve", t1[:, c, 0:nseq], a3[:, :, 0], h0s[:, c, 0:nseq], ALU.mult, r=["A", "h0s", "t1"], w=["t1"])
                    P.tt("dve", u3[:, :, 0], u3[:, :, 0], t1[:, c, 0:nseq], ALU.add, r=["UU", "t1"], w=["UU"])
                    P.memset("dve", a3[:, :, 0:1], 0.0, w=["A"])
                    P.scan("dve", H[:, c, :n], A_[:, c, :n], UU[:, c, :n], 0.0, r=["A", "UU"], w=["H"])
            if mode == "p":
                if last:
                    for c in range(2):
                        P.dma(O["rglru_h_prompt"][l][:, c * 128:(c + 1) * 128].rearrange("o p -> p o"), H[:, c, n - 1:n],
                              r=["H"], w=["o_h"], **kw)
            else:
                for c in range(2):
                    P.dma(O["rglru_h_sample"][l][:, c * 128:(c + 1) * 128].rearrange("s p -> p s"),
                          H[:, c, :n].rearrange("p (s t) -> p s t", s=nseq)[:, :, T - 1], r=["H"], w=["o_h"], **kw)
            gd = tl["gd"]
            for c in range(2):
                P.tt("pool", t1[:, c, :n], gd[:, c, :n], gd[:, c, :n], ALU.mult, r=["gd", "t1"], w=["t1"])
                P.ts("dve", t1[:, c, :n], t1[:, c, :n], 0.044715, 1.0, ALU.mult, ALU.add, r=["t1"], w=["t1"])
                P.tt("dve", t1[:, c, :n], t1[:, c, :n], gd[:, c, :n], ALU.mult, r=["t1", "gd"], w=["t1"])
                P.act(t1[:, c, :n], t1[:, c, :n], AF.Sigmoid, r=["t1"], w=["t1"], scale=1.5957691216057308)
                P.tt("dve", t1[:, c, :n], t1[:, c, :n], gd[:, c, :n], ALU.mult, r=["t1", "gd"], w=["t1"])
                P.tt("dve", H[:, c, :n], H[:, c, :n], t1[:, c, :n], ALU.mult, r=["H", "t1"], w=["H"])
            y = rms_T_store(H, "H", n, tl["R"], "R")
            for c in range(2):
                P.dma(YT[768 + c * 128:768 + (c + 1) * 128, c0:c0 + n], y[:, c, :n], r=["R"], w=["YTd"])

        SUB = cfg.get("sub", "cd,gla,fox,foxs").split(",")
        nblk = (L + 511) // 512
        for b_ in range(nblk if "cd" in SUB else 0):
            c0 = b_ * 512
            T = min(512, L - c0)
            mix_cd(c0, 1, T, "p", b_ == 0, b_ == nblk - 1)
        if "cd" in SUB:
            mix_cd(L, NS, 4, "s", True, True)

        PARTS = [(0, 96), (96, 32)]
        HP = [(0, 0), (0, 32), (0, 64), (1, 0)]
        def two(shape, dt=F32):
            return [m.get(shape, dt) for _ in range(2)]
        QT = two([128, 128]); KT = two([128, 128]); RAT = m.get([16, 128]); VG = m.get([128, 512])
        LA = two([128, 128]); Bc = two([128, 128]); EB = two([128, 128]); KE = two([128, 128])
        QTs = two([128, 128], BF16); KTs = two([128, 128], BF16); KENDR = m.get([128, 128], BF16)
        VB = m.get([128, 256], BF16); SCM = m.get([128, 4, 128], BF16)
        S = two([128, 64]); Sb = two([128, 64], BF16); EBL = two([128, 1]); nbap = two([128, 1])
        OS = m.get([128, 4, 64]); SQA = m.get([128, 4, 64]); SSA = m.get([128, 4]); SGA = m.get([128, 256]); YS = m.get([128, 2, 128])
        QTm = [m.get([128, 128], BF16) for _ in range(3)]
        hm = m.get([128, 3])
        P.memset("dve", hm, 0.0, w=["hm"])
        for h in range(3):
            P.memset("dve", hm[32 * h:32 * h + 32, h:h + 1], 1.0, w=["hm"])
        psc = pA[0].rearrange("p (h q) -> p h q", h=4)
        pso = pA[1][:, 0:256].rearrange("p (h d) -> p h d", h=4)
        psS0 = pA[1][:, 256:448]
        psS1 = pB[1][:, 256:320]
        psl = [pB[0][:, 0:128], pB[0][:, 128:256]]
        pst = pB[0][:, 256:384]
        pyt = pB[1][:, 0:256].rearrange("p (c q) -> p c q", c=2)
        for pi, (r0, nr) in enumerate(PARTS):
            P.dma(nbap[pi][:nr, :], I["b_alpha"][l][r0:r0 + nr].rearrange("(p o) -> p o", o=1), w=["nbap"], **kw)
            P.ts("dve", nbap[pi][:nr, :], nbap[pi][:nr, :], -1.0, None, ALU.mult, r=["nbap"], w=["nbap"])

        GSTOP = cfg.get("gstop", 9)

        def gla_chunk(c0, n):
            P.dma(RAT[:, :n], ZT[C_RA:C_RA + 16, c0:c0 + n], w=["RAT"])
            P.dma(VG[:n, :], Z[c0:c0 + n, C_VA:C_VA + 512], w=["VG"])
            for pi, (r0, nr) in enumerate(PARTS):
                P.dma(QT[pi][:nr, :n], ZT[C_QA + r0:C_QA + r0 + nr, c0:c0 + n], w=["QT"])
                P.dma(KT[pi][:nr, :n], ZT[C_KA + r0:C_KA + r0 + nr, c0:c0 + n], w=["KT"])
            for pi, (r0, nr) in enumerate(PARTS):
                P.mm(psl[pi][:nr, :n], wal[:, r0:r0 + nr], RAT[:, :n], r=["prm", "RAT"], w=["pB0"])
                la, bc, eb, ke = LA[pi][:nr, :n], Bc[pi][:nr, :n], EB[pi][:nr, :n], KE[pi][:nr, :n]
                P.act(la, psl[pi][:nr, :n], AF.Exp, r=["pB0", "nbap"], w=["LA"], scale=-1.0, bias=nbap[pi][:nr, 0:1])
                P.act(la, la, AF.Ln, r=["LA"], w=["LA"], bias=onec[:nr, 0:1])
                P.ts("dve", la, la, -1.0 / 16.0, None, ALU.mult, r=["LA"], w=["LA"])
                P.scan("dve", bc, ones_f[:nr, :n], la, 0.0, r=["LA", "ones_f"], w=["Bc"])
                P.act(eb, bc, AF.Exp, r=["Bc"], w=["EB"])
                P.stt("dve", QTs[pi][:nr, :n], QT[pi][:nr, :n], 32 ** -0.5, eb, ALU.mult, ALU.mult, r=["QT", "EB"], w=["QTs"])
                P.act(eb, bc, AF.Exp, r=["Bc", "QTs"], w=["EB"], scale=-1.0)
                P.tt("dve", KTs[pi][:nr, :n], KT[pi][:nr, :n], eb, ALU.mult, r=["KT", "EB"], w=["KTs"])
                P.act(ke, bc, AF.Exp, r=["Bc"], w=["KE"], scale=-1.0, bias=Bc[pi][:nr, n - 1:n])
                P.tt("dve", ke, ke, KT[pi][:nr, :n], ALU.mult, r=["KE", "KT"], w=["KE"])
                P.act(EBL[pi][:nr, 0:1], Bc[pi][:nr, n - 1:n], AF.Exp, r=["Bc"], w=["EBL"])
                P.tr(pst[:n, r0:r0 + nr], ke, ident_f[:nr, :nr], r=["KE", "ident_f"], w=["pB0"])
            if GSTOP <= 0:
                return
            P.copy("act", KENDR[:n, :], pst[:n, :], r=["pB0"], w=["KENDR"])
            P.copy("pool", VB[:n, :], VG[:n, 0:256], r=["VG"], w=["VB"])
            if GSTOP <= 1:
                return
            for h in range(3):
                P.ts("pool" if h == 1 else "dve", QTm[h][:96, :n], QTs[0][:96, :n], hm[:96, h:h + 1], None, ALU.mult,
                     r=["QTs", "hm"], w=["QTm"])
            for h in range(4):
                if h < 3:
                    P.mm(psc[:n, h, :n], KTs[0][0:96, :n], QTm[h][0:96, :n], r=["KTs", "QTm"], w=["pA0"])
                else:
                    P.mm(psc[:n, h, :n], KTs[1][0:32, :n], QTs[1][0:32, :n], r=["KTs", "QTs"], w=["pA0"])
            for h in range(4):
                P.tt("dve", SCM[:n, h, :n], psc[:n, h, :n], mask_qk[:n, :n], ALU.mult, r=["pA0", "mask_qk"], w=["SCM"])
            if GSTOP <= 2:
                return
            for h in range(4):
                pi, pb = HP[h]
                P.mm(pso[:n, h, :], SCM[:n, h, :n], VB[:n, 64 * h:64 * h + 64], start=True, stop=False, r=["SCM", "VB"], w=["pA1"])
                if h < 3:
                    P.mm(pso[:n, h, :], QTm[h][0:96, :n], Sb[0][0:96, :], start=False, stop=True, r=["QTm", "Sb"], w=["pA1"])
                else:
                    P.mm(pso[:n, h, :], QTs[1][0:32, :n], Sb[1][0:32, :], start=False, stop=True, r=["QTs", "Sb"], w=["pA1"])
            if GSTOP <= 3:
                return
            P.mm(psS0[0:96, :], KENDR[:n, 0:96], VB[:n, 0:192], r=["KENDR", "VB"], w=["pA1"])
            P.mm(psS1[0:32, :], KENDR[:n, 96:128], VB[:n, 192:256], r=["KENDR", "VB"], w=["pB1"])
            for h in range(4):
                pi, pb = HP[h]
                src = psS0[pb:pb + 32, 64 * h:64 * h + 64] if pi == 0 else psS1[0:32, :]
                P.stt("dve", S[pi][pb:pb + 32, :], S[pi][pb:pb + 32, :], EBL[pi][pb:pb + 32, 0:1], src, ALU.mult, ALU.add,
                      r=["S", "EBL", "pA1", "pB1"], w=["S"])
            for pi, (r0, nr) in enumerate(PARTS):
                P.copy("pool", Sb[pi][:nr, :], S[pi][:nr, :], r=["S"], w=["Sb"])
            if GSTOP <= 4:
                return
            P.copy("act", OS[:n], pso[:n], r=["pA1"], w=["OS"])
            P.tt("pool", SQA[:n], OS[:n], OS[:n], ALU.mult, r=["OS"], w=["SQA"])
            P.op("dve", lambda e: e.tensor_reduce(out=SSA[:n, :], in_=SQA[:n], axis=AX.X, op=ALU.add), r=["SQA"], w=["SSA"])
            P.act(SSA[:n, :], SSA[:n, :], AF.Sqrt, r=["SSA"], w=["SSA"], bias=epsc[:n, 0:1], scale=1.0 / 64)
            P.op("dve", lambda e: e.reciprocal(out=SSA[:n, :], in_=SSA[:n, :]), r=["SSA"], w=["SSA"])
            P.act(SGA[:n, :], VG[:n, 256:512], AF.Silu, r=["VG"], w=["SGA"])
            for h in range(4):
                P.stt("dve", OS[:n, h, :], OS[:n, h, :], SSA[:n, h:h + 1], SGA[:n, 64 * h:64 * h + 64], ALU.mult, ALU.mult,
                      r=["OS", "SSA", "SGA"], w=["OS"])
            for c in range(2):
                P.tr(pyt[:, c, :n], OS[:n, 2 * c:2 * c + 2, :].rearrange("p h d -> p (h d)"), ident_f[:n, :n],
                     r=["OS", "ident_f"], w=["pB1"])
            P.copy("act", YS[:, :, :n], pyt[:, :, :n], r=["pB1"], w=["YS"])
            for c in range(2):
                P.dma(YT[c * 128:(c + 1) * 128, c0:c0 + n], YS[:, c, :n], r=["YS"], w=["YTa"])

        for pi, (r0, nr) in enumerate(PARTS):
            P.memset("dve", S[pi][:, :], 0.0, w=["S"])
            P.memset("pool", Sb[pi][:, :], 0.0, w=["Sb"])
        for t in range(NTILE if "gla" in SUB else 0):
            gla_chunk(t * 128, tile_rows(t))
        for pi, (r0, nr) in enumerate(PARTS):
            P.dma(O["gla_prompt"][l][r0:r0 + nr, :], S[pi][:nr, :], r=["S"], w=["o_gla"])
        for b_ in range(NS if "gla" in SUB else 0):
            for pi, (r0, nr) in enumerate(PARTS):
                P.dma(S[pi][:nr, :], I["state_gla"][l, b_][r0:r0 + nr, :], w=["S"])
                P.copy("pool", Sb[pi][:nr, :], S[pi][:nr, :], r=["S"], w=["Sb"])
            gla_chunk(L + 4 * b_, 4)
            for pi, (r0, nr) in enumerate(PARTS):
                P.dma(O["gla_sample"][l, b_][r0:r0 + nr, :], S[pi][:nr, :], r=["S"], w=["o_gla"])

        mb = Carver(m.off)
        NTS = NTILE + 1
        KT2 = mb.get([128, 2, NTILE * 128], BF16); QT2 = mb.get([128, 2, 128], BF16)
        VA = mb.get([128, NTILE, 4, 65], BF16)
        LFR = mb.get([128, NTS, 4]); CW = mb.get([128, NTS, 4]); TOT = mb.get([128, NTS, 4]); INC = mb.get([128, NTS, 4])
        TOTh = mb.get([128, 4, NTS]); INCh = mb.get([128, 4, NTS]); Ch = mb.get([128, 4, NTS])
        BIAS = mb.get([128, NTILE]); stg = mb.get([128, 2, 512]); vst = mb.get([128, 256])
        PTb = [mb.get([128, 128], BF16) for _ in range(2)]
        QT2m = [mb.get([128, 128], BF16) for _ in range(4)]
        Ep = [mb.get([128, 128]) for _ in range(2)]
        OB = mb.get([128, 4, 64]); RB = mb.get([128, 4]); SQB = mb.get([128, 256]); YSB = mb.get([128, 2, 128])
        pss_ = [pO[0], pO[1]]
        pob = pA[0][:, 0:65]
        pcw = pB[0]; ptot = pB[1]

        P.memset("dve", LFR, 0.0, w=["LFR"])
        for t in range(NTILE):
            P.dma(LFR[:tile_rows(t), t, :], Z[t * 128:t * 128 + tile_rows(t), C_FB:C_FB + 4], r=["LFR"], w=["LFR"], **kw)
        P.dma(LFR[:NT, NTILE, :], Z[L:LT, C_FB:C_FB + 4], r=["LFR"], w=["LFR"], **kw)
        for h in range(4):
            P.ts("dve", CW[:, :, h], LFR[:, :, h], bfr[:, h:h + 1], None, ALU.add, r=["LFR", "prm"], w=["CW"])
        P.act(CW, CW, AF.Exp, r=["CW"], w=["CW"], scale=-1.0)
        P.act(CW, CW, AF.Ln, r=["CW"], w=["CW"], bias=onec[:, 0:1])
        P.memset("dve", LFR, 0.0, w=["LFR"])
        for t in range(NTILE):
            P.ts("dve", LFR[:tile_rows(t), t, :], CW[:tile_rows(t), t, :], -1.0, None, ALU.mult, r=["CW", "LFR"], w=["LFR"])
        P.ts("dve", LFR[:NT, NTILE, :], CW[:NT, NTILE, :], -1.0, None, ALU.mult, r=["CW", "LFR"], w=["LFR"])
        for t in range(NTILE):
            P.dma(O["logf_prompt"][l][t * 128:t * 128 + tile_rows(t), :], LFR[:tile_rows(t), t, :], r=["LFR"], w=["o_lf"], **kw)
        P.dma(O["logf_sample"][l], LFR[:NT, NTILE, :], r=["LFR"], w=["o_lfs"], **kw)
        ncol = NTILE * 4
        LFf = LFR[:, 0:NTILE, :].rearrange("p t h -> p (t h)")
        P.mm(pcw[:, :ncol], tri_le[:, :], LFf, r=["tri_le", "LFR"], w=["pB0"])
        P.mm(ptot[:, :ncol], ones_f[:, :], LFf, r=["ones_f", "LFR"], w=["pB1"])
        pcw3 = pcw[:, :ncol].rearrange("p (t h) -> p t h", h=4)
        ptot3 = ptot[:, :ncol].rearrange("p (t h) -> p t h", h=4)
        P.copy("act", TOTh[:, :, 0:NTILE].rearrange("p h t -> p t h"), ptot3, r=["pB1"], w=["TOT"])
        for h in range(4):
            P.scan("dve", INCh[:, h, 0:NTILE], ones_f[:, 0:NTILE], TOTh[:, h, 0:NTILE], 0.0, r=["TOT", "ones_f"], w=["INC"])
        P.tt("dve", Ch[:, :, 0:NTILE].rearrange("p h t -> p t h"), pcw3, INCh[:, :, 0:NTILE].rearrange("p h t -> p t h"), ALU.add,
             r=["pB0", "INC"], w=["Ch"])
        P.tt("dve", Ch[:, :, 0:NTILE], Ch[:, :, 0:NTILE], TOTh[:, :, 0:NTILE], ALU.subtract, r=["Ch", "TOT"], w=["Ch"])

        P.memset("pool", VA[:, :, :, 64:65], 1.0, w=["VA"])
        for t0 in range(0, L, 512):
            n = min(512, L - t0)
            for c in range(2):
                P.dma(stg[:, c, :n], ZT[C_KB + c * 128:C_KB + (c + 1) * 128, t0:t0 + n], w=["stg"])
            for c in range(2):
                P.copy("dve", KT2[:, c, t0:t0 + n], stg[:, c, :n], r=["stg"], w=["KT2"])
        for t in range(NTILE):
            rws = tile_rows(t)
            P.dma(vst[:rws, :], Z[t * 128:t * 128 + rws, C_VB:C_VB + 256], w=["vst"])
            P.copy("pool", VA[:rws, t, :, 0:64], vst[:rws, :].rearrange("p (h d) -> p h d", h=4), r=["vst", "VA"], w=["VA"])
        P.dma(O["k_prompt"][l], Z[0:L, C_KB:C_KB + 256], w=["o_k"])
        P.dma(O["v_prompt"][l], Z[0:L, C_VB:C_VB + 256], w=["o_v"])
        P.dma(O["k_sample"][l], Z[L:LT, C_KB:C_KB + 256], w=["o_k"])
        P.dma(O["v_sample"][l], Z[L:LT, C_VB:C_VB + 256], w=["o_v"])

        unit = 0
        for i in range(NTILE if "fox" in SUB else 0):
            rq = tile_rows(i)
            for c in range(2):
                P.dma(stg[:, c, :rq], ZT[C_QB + c * 128:C_QB + (c + 1) * 128, i * 128:i * 128 + rq], w=["stg"])
            for h in range(4):
                P.ts("dve", QT2m[h][:, :rq], stg[:, h // 2, :rq], hm2[:, h % 2:h % 2 + 1], None, ALU.mult,
                     r=["stg", "hm2"], w=["QT2"])
            for h in range(4):
                hp, hc_ = 64 * (h % 2), h // 2
                P.ts("dve", BIAS[:, 0:i + 1], Ch[:, h, 0:i + 1], -1.0, INCh[:, h, i:i + 1], ALU.mult, ALU.add,
                     r=["Ch", "INC"], w=["BIAS"])
                for j in range(i + 1):
                    rk = tile_rows(j)
                    pp = pss_[unit % 2]
                    pt = PTb[unit % 2]
                    ptok = "pO%d" % (unit % 2)
                    P.mm(pp[:rk, :rq], KT2[:, hc_, j * 128:j * 128 + rk], QT2m[h][:, :rq],
                         r=["KT2", "QT2"], w=[ptok])
                    P.act(pt[:rk, :rq], pp[:rk, :rq], AF.Exp, r=[ptok, "BIAS"], w=[("PTb", unit % 2)],
                          bias=BIAS[:rk, j:j + 1])
                    if j == i:
                        P.tt("dve", pt[:rk, :rq], pt[:rk, :rq], mask_qk[:rk, :rq], ALU.mult,
                             r=[("PTb", unit % 2), "mask_qk"], w=[("PTb", unit % 2)])
                    P.mm(pob[:rq, :], pt[:rk, :rq], VA[:rk, j, h, :], start=(j == 0), stop=(j == i),
                         r=[("PTb", unit % 2), "VA"], w=["pA0"])
                    unit += 1
                P.op("dve", lambda e, rq=rq, h=h: e.reciprocal(out=RB[:rq, h:h + 1], in_=pob[:rq, 64:65]), r=["pA0"], w=["RB"])
                P.ts("dve", OB[:rq, h, :], pob[:rq, 0:64], RB[:rq, h:h + 1], None, ALU.mult, r=["pA0", "RB"], w=["OB"])
            fox_finish(OB, rq, i * 128, SQB, RB, YSB)

        P.barrier()
        if "foxs" in SUB:
            fox_sample(l, Carver(prm_end), CW, INC, LFR, NTILE)
        P.barrier()

    def fox_finish(OB, rq, c0, SQB, RB, YSB):
        pyt = pB[1][:, 0:256].rearrange("p (c q) -> p c q", c=2)
        OBf = OB[:rq].rearrange("p h d -> p (h d)")
        P.memset("dve", RB[:rq, 0:1], 0.0, w=["RB"])
        P.act(SQB[:rq, :], OBf, AF.Square, r=["OB", "RB"], w=["SQB", "RB"], accum_out=RB[:rq, 0:1])
        P.act(RB[:rq, 0:1], RB[:rq, 0:1], AF.Sqrt, r=["RB"], w=["RB"], bias=epsc[:rq, 0:1], scale=1.0 / 256)
        P.op("dve", lambda e: e.reciprocal(out=RB[:rq, 0:1], in_=RB[:rq, 0:1]), r=["RB"], w=["RB"])
        P.ts("dve", SQB[:rq, :], OBf, RB[:rq, 0:1], None, ALU.mult, r=["OB", "RB", "SQB"], w=["SQB"])
        for c in range(2):
            P.tr(pyt[:, c, :rq], SQB[:rq, c * 128:(c + 1) * 128], ident_f[:rq, :rq], r=["SQB", "ident_f"], w=["pB1"])
        P.copy("act", YSB[:, :, :rq], pyt[:, :, :rq], r=["pB1"], w=["YSB"])
        for c in range(2):
            P.dma(YT[256 + c * 128:256 + (c + 1) * 128, c0:c0 + rq], YSB[:, c, :rq], r=["YSB"], w=["YTb"])

    def fox_sample(l, mb, CW, INC, LFR, NTILE):
        kw = dict(allow_slow_non_contiguous=True)
        NPGS = NPG + 1
        KP = mb.get([128, NPG, 256]); VP = mb.get([128, NPG, 256])
        KTs = mb.get([128, 2, NPGS, 128], BF16); QTsm = mb.get([128, 4, 4], BF16); qst = mb.get([128, 2, 4])
        VAs = mb.get([128, NPGS, 4, 65], BF16); vn = mb.get([4, 256])
        nrow = NS * NPG
        nhalf = (nrow + 127) // 128
        LFP = mb.get([128, nhalf, 512]); PTc = mb.get([128, nhalf], I32)
        PTI = mb.get([128, nrow], I32); IDX = mb.get([128, nrow], I32)
        LFs = mb.get([128, NS, NPGS, 4]); SFX = mb.get([128, NS, NPGS, 4]); TOs = mb.get([128, NS, NPGS, 4]); OSF = mb.get([128, NS, NPGS, 4])
        Es = mb.get([128, NPGS, 4]); PTs = mb.get([128, NPGS, 4], BF16)
        OBs = mb.get([4, 4, 64]); RBs = mb.get([4, 4]); SQs = mb.get([4, 256]); YSs = mb.get([128, 2, 4])
        ptr = pB[0]; pq = pO[0][:, 0:NPGS * 4].rearrange("p (j q) -> p j q", q=4); pos = pO[1][:, 0:65]
        P.dma(PTI[:, :], I["page_table"][0:1, :].to_broadcast([128, nrow]), w=["PTI"])
        P.ts("dve", IDX[:, :], PTI[:, :], 128.0, iota_f[:, 0:1], ALU.mult, ALU.add, r=["PTI", "iota_f"], w=["IDX"])
        for hf in range(nhalf):
            nr = min(128, nrow - hf * 128)
            P.dma(PTc[:nr, hf:hf + 1], I["page_table"][0:1, hf * 128:hf * 128 + nr].rearrange("o p -> p o"), w=["PTc"], **kw)
            P.op("pool", lambda e, hf=hf, nr=nr: e.indirect_dma_start(
                out=LFP[:nr, hf, :], out_offset=None, in_=I["cache_logf"].rearrange("l n w -> (l n) w"),
                in_offset=bass.IndirectOffsetOnAxis(ap=PTc[:nr, hf:hf + 1], axis=0),
                element_offset=l * NPOOL * 512), r=["PTc"], w=["LFP"], dma=True)
        P.memset("dve", LFs, 0.0, w=["LFs"])
        for hf in range(nhalf):
            nr = min(128, nrow - hf * 128)
            nb = nr // NPG
            for h in range(4):
                P.tr(ptr[:, h * 128:h * 128 + nr], LFP[:nr, hf, :].rearrange("p (s h) -> p s h", h=4)[:, :, h], ident_f[:nr, :nr],
                     r=["LFP", "ident_f"], w=["pB0"])
            for h in range(4):
                P.copy("dve", LFs[:, hf * (128 // NPG):hf * (128 // NPG) + nb, 0:NPG, h],
                       ptr[:, h * 128:h * 128 + nr].rearrange("p (b j) -> p b j", j=NPG), r=["pB0", "LFs"], w=["LFs"])
        P.dma(LFs[0:4, :, NPG, :], O["logf_sample"][l].rearrange("(b t) h -> t b h", t=4), r=["o_lfs", "LFs"], w=["LFs"], **kw)
        ncol = NS * NPGS * 4
        LFsf = LFs.rearrange("p b j h -> p (b j h)")
        SFXf = SFX.rearrange("p b j h -> p (b j h)")
        TOsf = TOs.rearrange("p b j h -> p (b j h)")
        c = 0
        while c < ncol:
            w_ = min(512, ncol - c)
            P.mm(pB[0][:, :w_], tri_gt[:, :], LFsf[:, c:c + w_], r=["tri_gt", "LFs"], w=["pB0"])
            P.copy("act", SFXf[:, c:c + w_], pB[0][:, :w_], r=["pB0"], w=["SFX"])
            P.mm(pB[1][:, :w_], ones_f[:, :], LFsf[:, c:c + w_], r=["ones_f", "LFs"], w=["pB1"])
            P.copy("act", TOsf[:, c:c + w_], pB[1][:, :w_], r=["pB1"], w=["TOs"])
            c += w_
        P.memset("dve", OSF[:, :, NPG, :], 0.0, w=["OSF"])
        for j in range(NPG - 1, -1, -1):
            P.tt("dve", OSF[:, :, j, :], OSF[:, :, j + 1, :], TOs[:, :, j + 1, :], ALU.add, r=["OSF", "TOs"], w=["OSF"])
        P.tt("dve", SFX, SFX, OSF, ALU.add, r=["SFX", "OSF"], w=["SFX"])
        P.memset("pool", VAs[:, :, :, 64:65], 1.0, w=["VAs"])
        for b in range(NS):
            tb = L + 4 * b
            for j in range(NPG):
                col = b * NPG + j
                P.op("pool", lambda e, j=j, col=col: e.indirect_dma_start(
                    out=KP[:, j, :], out_offset=None, in_=I["cache_k"].rearrange("l n w -> (l n) w"),
                    in_offset=bass.IndirectOffsetOnAxis(ap=IDX[:, col:col + 1], axis=0),
                    element_offset=l * NPOOL * 128 * 256), r=["IDX"], w=["KP"], dma=True)
                P.op("pool", lambda e, j=j, col=col: e.indirect_dma_start(
                    out=VP[:, j, :], out_offset=None, in_=I["cache_v"].rearrange("l n w -> (l n) w"),
                    in_offset=bass.IndirectOffsetOnAxis(ap=IDX[:, col:col + 1], axis=0),
                    element_offset=l * NPOOL * 128 * 256), r=["IDX"], w=["VP"], dma=True)
            for c_ in range(2):
                for j0 in range(0, NPG, 4):
                    for jj in range(4):
                        P.tr(ptr[:, jj * 128:(jj + 1) * 128], KP[:, j0 + jj, c_ * 128:(c_ + 1) * 128], ident_f[:, :],
                             r=["KP", "ident_f"], w=["pB0"])
                    P.copy("act" if (j0 // 4) % 2 else "dve", KTs[:, c_, j0:j0 + 4, :],
                           ptr[:, 0:512].rearrange("p (j s) -> p j s", j=4), r=["pB0"], w=["KTs"])
                P.dma(KTs[:, c_, NPG, 0:4], ZT[C_KB + c_ * 128:C_KB + (c_ + 1) * 128, tb:tb + 4], r=["KTs"], w=["KTs"], q="pool")
                P.dma(qst[:, c_, :], ZT[C_QB + c_ * 128:C_QB + (c_ + 1) * 128, tb:tb + 4], w=["qst"])
            for h in range(4):
                P.ts("dve", QTsm[:, h, :], qst[:, h // 2, :], hm2[:, h % 2:h % 2 + 1], None, ALU.mult, r=["qst", "hm2"], w=["QTs"])
            P.copy("pool", VAs[:, 0:NPG, :, 0:64], VP.rearrange("p j (h d) -> p j h d", h=4), r=["VP", "VAs"], w=["VAs"])
            P.dma(vn[:, :], Z[tb:tb + 4, C_VB:C_VB + 256], w=["vn"])
            P.copy("pool", VAs[0:4, NPG, :, 0:64], vn.rearrange("p (h d) -> p h d", h=4), r=["vn", "VAs"], w=["VAs"])
            for h in range(4):
                hp, hc_ = 64 * (h % 2), h // 2
                for j in range(NPGS):
                    rk = 128 if j < NPG else 4
                    P.mm(pq[:rk, j, :], KTs[:, hc_, j, :rk], QTsm[:, h, :], r=["KTs", "QTs"], w=["pO0"])
                P.tt("dve", Es[:, 0:NPG, :], pq[:, 0:NPG, :], SFX[:, b, 0:NPG, h:h + 1].to_broadcast([128, NPG, 4]), ALU.add,
                     r=["pO0", "SFX"], w=["Es"])
                P.tt("dve", Es[0:4, NPG, :], pq[0:4, NPG, :], SFX[0:4, b, NPG, h:h + 1].to_broadcast([4, 4]), ALU.add,
                     r=["pO0", "SFX", "Es"], w=["Es"])
                P.act(PTs[:, 0:NPG, :], Es[:, 0:NPG, :], AF.Exp, r=["Es"], w=["PTs"])
                P.act(PTs[0:4, NPG, :], Es[0:4, NPG, :], AF.Exp, r=["Es", "PTs"], w=["PTs"])
                P.op("pool", lambda e: e.affine_select(out=PTs[0:4, NPG, :], in_=PTs[0:4, NPG, :], pattern=[[1, 4]],
                                                       compare_op=ALU.is_ge, fill=0.0, base=0, channel_multiplier=-1),
                     r=["PTs"], w=["PTs"])
                for j in range(NPGS):
                    rk = 128 if j < NPG else 4
                    P.mm(pos[0:4, :], PTs[:rk, j, :], VAs[:rk, j, h, :], start=(j == 0), stop=(j == NPGS - 1),
                         r=["PTs", "VAs"], w=["pO1"])
                P.op("dve", lambda e, h=h: e.reciprocal(out=RBs[0:4, h:h + 1], in_=pos[0:4, 64:65]), r=["pO1"], w=["RBs"])
                P.ts("dve", OBs[0:4, h, :], pos[0:4, 0:64], RBs[0:4, h:h + 1], None, ALU.mult, r=["pO1", "RBs"], w=["OB"])
            fox_finish(OBs, 4, tb, SQs, RBs, YSs)

    def pass_mix_dummy(l):
        z = stage[:, 0, :]
        P.memset("pool", z, 0.0, w=[("stage", 0)])
        for (r0, n, subs) in groups:
            for k in range(8):
                P.dma(YT[k * 128:(k + 1) * 128, r0:r0 + n], z[:, :n], r=[("stage", 0)], w=[("YT", r0)])

    pass_init()
    for l in range(DEPTH):
        pass_ffn(l, 1)
        pass_in(l)
        if cfg.get("mix", "full") == "dummy":
            pass_mix_dummy(l)
        else:
            pass_mix(l)
        pass_out(l)
        pass_ffn(l, 2)
    pass_final()

    P.emit(st)
    st.close()
    return nc


def _core_inputs(inp, prompt_idx, s0, NS, DEPTH):
    f = lambda a: np.ascontiguousarray(np.asarray(a))
    npool = inp["cache_k"].shape[1]
    d = {}
    d["x_prompt"] = f(inp["x_prompt"][prompt_idx])
    d["x_sample"] = f(np.asarray(inp["x_sample"])[s0:s0 + NS].reshape(NS * 4, D))
    d["meta_tokens"] = f(inp["meta_tokens"])
    d["cache_k"] = f(np.asarray(inp["cache_k"]).reshape(DEPTH, npool * 128, 256))
    d["cache_v"] = f(np.asarray(inp["cache_v"]).reshape(DEPTH, npool * 128, 256))
    d["cache_logf"] = f(np.asarray(inp["cache_logf"]).reshape(DEPTH, npool, 512))
    d["state_gla"] = f(np.asarray(inp["state_gla"])[:, s0:s0 + NS].reshape(DEPTH, NS, 128, 64))
    d["state_conv_c"] = f(np.asarray(inp["state_conv_c"])[:, s0:s0 + NS])
    d["state_rglru_h"] = f(np.asarray(inp["state_rglru_h"])[:, s0:s0 + NS])
    d["state_conv_d"] = f(np.asarray(inp["state_conv_d"])[:, s0:s0 + NS])
    d["page_table"] = f(np.asarray(inp["page_table"])[s0:s0 + NS].reshape(1, -1).astype(np.int32))
    for nm in ("ln_ffn1", "ln_mix", "g_norm", "ln_ffn2", "ffn1_gate", "ffn1_up", "ffn2_gate", "ffn2_up", "ffn1_down",
               "ffn2_down", "w_in", "w_out", "w_alpha_up", "b_alpha", "b_forget", "conv_c_w", "conv_d_w", "conv_d_b",
               "lru_b_r", "lru_b_i", "lru_lambda", "lru_w_r", "lru_w_i"):
        d[nm] = f(inp[nm])
    d["ln_final"] = f(np.asarray(inp["ln_final"]).reshape(1, D))
    return d


def run_cores(inp, n_cores, NS, prompt_of_core, cfg_extra=None):
    DEPTH = inp["w_in"].shape[0]
    seq = inp["x_prompt"].shape[1]
    cfg = dict(L=seq + 16, NS=NS, depth=DEPTH, n_pool=inp["cache_k"].shape[1], n_pages=inp["page_table"].shape[1])
    if cfg_extra:
        cfg.update(cfg_extra)
    nc = build(cfg)
    maps = [_core_inputs(inp, prompt_of_core[c], c * NS, NS, DEPTH) for c in range(n_cores)]
    res = run_bass_kernel_spmd(nc, maps, core_ids=list(range(n_cores)))
    return res.results, cfg


def assemble(results, cfg, n_prompt, n_cores):
    L, NS, DEPTH = cfg["L"], cfg["NS"], cfg["depth"]
    R = results
    cat_p = lambda nm, shp: np.stack([R[b][nm].reshape(shp) for b in range(n_prompt)], axis=0)
    y_prompt = cat_p("y_prompt", (L - 16, D))
    y_sample = np.concatenate([R[c]["y_sample"].reshape(NS, 4, D) for c in range(n_cores)], axis=0)
    stp = lambda nm, shp: np.stack([R[b][nm].reshape((DEPTH,) + shp) for b in range(n_prompt)], axis=1)
    sts = lambda nm, shp: np.concatenate([R[c][nm].reshape((DEPTH, NS) + shp) for c in range(n_cores)], axis=1)
    return (y_prompt, y_sample,
            stp("k_prompt", (L, 4, 64)), stp("v_prompt", (L, 4, 64)), stp("logf_prompt", (L, 4)),
            stp("gla_prompt", (4, 32, 64)), stp("conv_c_prompt", (2, 256)), stp("rglru_h_prompt", (256,)),
            stp("conv_d_prompt", (3, 256)),
            sts("k_sample", (4, 4, 64)), sts("v_sample", (4, 4, 64)), sts("logf_sample", (4, 4)),
            sts("gla_sample", (4, 32, 64)), sts("conv_c_sample", (2, 256)), sts("rglru_h_sample", (256,)),
            sts("conv_d_sample", (3, 256)))


def kernel(**inputs):
    n_cores = 8
    nb = inputs["x_prompt"].shape[0]
    ns_total = inputs["x_sample"].shape[0]
    NS = ns_total // n_cores
    prompt_of_core = [c if c < nb else 0 for c in range(n_cores)]
    results, cfg = run_cores(inputs, n_cores, NS, prompt_of_core)
    outs = assemble(results, cfg, nb, n_cores)
    return tuple(np.ascontiguousarray(o.astype(np.float32)) for o in outs)
```

```python
import numpy as np
import concourse.bass as bass
import concourse.mybir as mybir
from concourse.bass_utils import run_bass_kernel_spmd

F32 = mybir.dt.float32
BF16 = mybir.dt.bfloat16
I32 = mybir.dt.int32
AF = mybir.ActivationFunctionType
ALU = mybir.AluOpType
AX = mybir.AxisListType

D = 1024
DFF = 2816
DIN = 2836
EPS = 1e-6
C_QA, C_KA, C_VA, C_GA, C_RA = 0, 128, 256, 512, 768
C_QB, C_KB, C_VB, C_FB = 784, 1040, 1296, 1552
C_BC, C_CC, C_HC, C_GD, C_XD = 1556, 1812, 2068, 2324, 2580


class Prog:
    def __init__(self, nc, nslots=14):
        self.nc = nc
        self.ins = {e: [] for e in ("pe", "act", "dve", "pool", "sp")}
        self.cnt = {e: 0 for e in self.ins}
        self.lastw = {}
        self.rd = {}
        self.waited = {e: {} for e in self.ins}
        self.nslots = nslots
        self.slot_uses = {}
        self.slot_next = {}
        self.keys = set()
        self.bar = {e: {} for e in self.ins}

    def barrier(self):
        evs = {}
        for e in self.ins:
            if self.cnt[e] > 0:
                evs[("e", e)] = self.cnt[e]
        for k, u in self.slot_uses.items():
            evs[k] = 16 * u
        for e in self.ins:
            for k, v in evs.items():
                if self.bar[e].get(k, 0) < v:
                    self.bar[e][k] = v

    def op(self, eng, fn, r=(), w=(), dma=False):
        d = dict(self.bar[eng])
        self.bar[eng] = {}

        def add(k, v):
            if d.get(k, 0) < v:
                d[k] = v

        for t in r:
            if t in self.lastw:
                add(*self.lastw[t])
        for t in w:
            if t in self.lastw:
                add(*self.lastw[t])
            for k, v in self.rd.get(t, {}).items():
                add(k, v)
        if dma:
            slot = self.slot_next.get(eng, 0)
            self.slot_next[eng] = (slot + 1) % self.nslots
            key = ("d", eng, slot)
            uses = self.slot_uses.get(key, 0)
            if uses > 0:
                add(key, 16 * uses)
            self.slot_uses[key] = uses + 1
            ev = (key, 16 * (uses + 1))
            inc = 16
        else:
            self.cnt[eng] += 1
            key = ("e", eng)
            ev = (key, self.cnt[eng])
            inc = 1
        self.keys.add(key)
        waits = []
        wd = self.waited[eng]
        for k, v in d.items():
            if k == ("e", "pe") and eng == "pe":
                continue
            if wd.get(k, 0) >= v:
                continue
            wd[k] = v
            waits.append((k, v))
        self.ins[eng].append((waits, fn, key, inc))
        for t in w:
            self.lastw[t] = ev
            self.rd[t] = {}
        for t in r:
            rr = self.rd.setdefault(t, {})
            if rr.get(ev[0], 0) < ev[1]:
                rr[ev[0]] = ev[1]
        return ev

    def dma(self, out, in_, r=(), w=(), q="sp", **kw):
        return self.op(q, lambda e: e.dma_start(out=out, in_=in_, **kw), r, w, dma=True)

    def mm(self, out, lhsT, rhs, start=True, stop=True, r=(), w=()):
        return self.op("pe", lambda e: e.matmul(out, lhsT, rhs, start=start, stop=stop), r, w)

    def tr(self, out, in_, ident, r=(), w=()):
        return self.op("pe", lambda e: e.transpose(out, in_, ident), r, w)

    def act(self, out, in_, func, r=(), w=(), eng="act", **kw):
        return self.op(eng, lambda e: e.activation(out=out, in_=in_, func=func, **kw), r, w)

    def tt(self, eng, out, in0, in1, op, r=(), w=()):
        return self.op(eng, lambda e: e.tensor_tensor(out=out, in0=in0, in1=in1, op=op), r, w)

    def ts(self, eng, out, in0, s1, s2, op0, op1=None, r=(), w=()):
        if op1 is None:
            return self.op(eng, lambda e: e.tensor_scalar(out=out, in0=in0, scalar1=s1, scalar2=None, op0=op0), r, w)
        return self.op(eng, lambda e: e.tensor_scalar(out=out, in0=in0, scalar1=s1, scalar2=s2, op0=op0, op1=op1), r, w)

    def stt(self, eng, out, in0, scalar, in1, op0, op1, r=(), w=()):
        return self.op(eng, lambda e: e.scalar_tensor_tensor(out=out, in0=in0, scalar=scalar, in1=in1, op0=op0, op1=op1), r, w)

    def copy(self, eng, out, in_, r=(), w=()):
        if eng == "act":
            return self.op(eng, lambda e: e.copy(out=out, in_=in_), r, w)
        return self.op(eng, lambda e: e.tensor_copy(out=out, in_=in_), r, w)

    def memset(self, eng, ap, val, w=()):
        return self.op(eng, lambda e: e.memset(ap, val), (), w)

    def scan(self, eng, out, d0, d1, init, r=(), w=()):
        return self.op(eng, lambda e: e.tensor_tensor_scan(out=out, data0=d0, data1=d1, initial=init,
                                                           op0=ALU.mult, op1=ALU.add), r, w)

    def emit(self, stack):
        nc = self.nc
        sems = {}
        for i, k in enumerate(sorted(self.keys, key=str)):
            sems[k] = stack.enter_context(nc.semaphore("s%d" % i))
        finals = {}
        for k in self.keys:
            if k[0] == "e":
                finals[k] = self.cnt[k[1]]
            else:
                finals[k] = 16 * self.slot_uses[k]
        block = stack.enter_context(nc.Block())

        def mk(name, last=False):
            def f(eng):
                for waits, fn, key, inc in self.ins[name]:
                    for k, v in waits:
                        eng.wait_ge(sems[k], v)
                    fn(eng).then_inc(sems[key], inc)
                if last:
                    for k, v in finals.items():
                        if v > 0:
                            eng.wait_ge(sems[k], v)
            return f

        block.tensor(mk("pe"))
        block.scalar(mk("act"))
        block.vector(mk("dve"))
        block.gpsimd(mk("pool"))
        block.sync(mk("sp", last=True))


def build(cfg):
    from contextlib import ExitStack
    L = cfg["L"]
    NS = cfg["NS"]
    NT = NS * 4
    LT = L + NT
    DEPTH = cfg["depth"]
    NPOOL = cfg["n_pool"]
    NPG = cfg["n_pages"]
    stages = cfg.get("stages", "all")
    nc = bass.Bass("TRN2", target_bir_lowering=False)

    def din(name, shape, dt=F32):
        return nc.dram_tensor(name, list(shape), dt, kind="ExternalInput").ap()

    def dout(name, shape, dt=F32):
        return nc.dram_tensor(name, list(shape), dt, kind="ExternalOutput").ap()

    def dscr(name, shape, dt=F32):
        return nc.dram_tensor(name, list(shape), dt, kind="Internal").ap()

    I = {}
    I["x_prompt"] = din("x_prompt", [L - 16, D])
    I["x_sample"] = din("x_sample", [NT, D])
    I["meta_tokens"] = din("meta_tokens", [16, D])
    I["cache_k"] = din("cache_k", [DEPTH, NPOOL * 128, 256])
    I["cache_v"] = din("cache_v", [DEPTH, NPOOL * 128, 256])
    I["cache_logf"] = din("cache_logf", [DEPTH, NPOOL, 512])
    I["state_gla"] = din("state_gla", [DEPTH, NS, 128, 64])
    I["state_conv_c"] = din("state_conv_c", [DEPTH, NS, 2, 256])
    I["state_rglru_h"] = din("state_rglru_h", [DEPTH, NS, 256])
    I["state_conv_d"] = din("state_conv_d", [DEPTH, NS, 3, 256])
    I["page_table"] = din("page_table", [1, NS * NPG], I32)
    for nm in ("ln_ffn1", "ln_mix", "g_norm", "ln_ffn2"):
        I[nm] = din(nm, [DEPTH, D])
    for nm in ("ffn1_gate", "ffn1_up", "ffn2_gate", "ffn2_up"):
        I[nm] = din(nm, [DEPTH, D, DFF])
    for nm in ("ffn1_down", "ffn2_down"):
        I[nm] = din(nm, [DEPTH, DFF, D])
    I["w_in"] = din("w_in", [DEPTH, D, DIN])
    I["w_out"] = din("w_out", [DEPTH, D, D])
    I["w_alpha_up"] = din("w_alpha_up", [DEPTH, 16, 128])
    I["b_alpha"] = din("b_alpha", [DEPTH, 128])
    I["b_forget"] = din("b_forget", [DEPTH, 4])
    I["conv_c_w"] = din("conv_c_w", [DEPTH, 3, 256])
    I["conv_d_w"] = din("conv_d_w", [DEPTH, 4, 256])
    for nm in ("conv_d_b", "lru_b_r", "lru_b_i", "lru_lambda"):
        I[nm] = din(nm, [DEPTH, 256])
    I["lru_w_r"] = din("lru_w_r", [DEPTH, 4, 64, 64])
    I["lru_w_i"] = din("lru_w_i", [DEPTH, 4, 64, 64])
    I["ln_final"] = din("ln_final", [1, D])

    O = {}
    O["y_prompt"] = dout("y_prompt", [L - 16, D])
    O["y_sample"] = dout("y_sample", [NT, D])
    O["k_prompt"] = dout("k_prompt", [DEPTH, L, 256])
    O["v_prompt"] = dout("v_prompt", [DEPTH, L, 256])
    O["logf_prompt"] = dout("logf_prompt", [DEPTH, L, 4])
    O["gla_prompt"] = dout("gla_prompt", [DEPTH, 128, 64])
    O["conv_c_prompt"] = dout("conv_c_prompt", [DEPTH, 2, 256])
    O["rglru_h_prompt"] = dout("rglru_h_prompt", [DEPTH, 1, 256])
    O["conv_d_prompt"] = dout("conv_d_prompt", [DEPTH, 3, 256])
    O["k_sample"] = dout("k_sample", [DEPTH, NT, 256])
    O["v_sample"] = dout("v_sample", [DEPTH, NT, 256])
    O["logf_sample"] = dout("logf_sample", [DEPTH, NT, 4])
    O["gla_sample"] = dout("gla_sample", [DEPTH, NS, 128, 64])
    O["conv_c_sample"] = dout("conv_c_sample", [DEPTH, NS, 2, 256])
    O["rglru_h_sample"] = dout("rglru_h_sample", [DEPTH, NS, 256])
    O["conv_d_sample"] = dout("conv_d_sample", [DEPTH, NS, 3, 256])

    X = dscr("X", [LT, D])
    Z = dscr("Z", [LT, DIN])
    ZT = dscr("ZT", [DIN, LT])
    DBG = bool(cfg.get("dbg"))
    if DBG:
        O["dbg_pp"] = dout("dbg_pp", [16, 16])
        O["dbg_kt"] = dout("dbg_kt", [128, 16], BF16)
        O["dbg_qt"] = dout("dbg_qt", [128, 16], BF16)
    YT = (dout if cfg.get("dbg") else dscr)("YT", [D, LT])

    P = Prog(nc)
    st = ExitStack()

    def sb(name, shape, dt=F32):
        return st.enter_context(nc.sbuf_tensor(name, list(shape), dt))

    def ps(name, shape, dt=F32):
        return st.enter_context(nc.psum_tensor(name, list(shape), dt))

    ident_f = sb("ident_f", [128, 128])
    ident_b = sb("ident_b", [128, 128], BF16)
    ones_f = sb("ones_f", [128, 128])
    P.memset("pool", ones_f[:], 1.0, w=["ones_f"])
    P.memset("pool", ident_f[:], 0.0, w=["ident_f"])
    P.op("pool", lambda e: e.affine_select(out=ident_f[:], in_=ones_f[:], pattern=[[1, 128]],
                                           compare_op=ALU.is_equal, fill=0.0, base=0, channel_multiplier=-1),
         r=["ones_f"], w=["ident_f"])
    P.copy("dve", ident_b[:], ident_f[:], r=["ident_f"], w=["ident_b"])
    epsc = sb("epsc", [128, 1])
    P.memset("pool", epsc[:], EPS, w=["epsc"])

    groups = []
    r0 = 0
    while r0 < LT:
        n = min(512, LT - r0)
        subs = []
        o = 0
        while o < n:
            subs.append((o, min(128, n - o)))
            o += 128
        groups.append((r0, n, subs))
        r0 += n

    ARENA = 204800
    arena = sb("arena", [128, ARENA // 2], BF16)

    class Carver:
        def __init__(self, off=0):
            self.off = off

        def get(self, shape, dt=F32):
            esz = 4 if dt in (F32, I32) else 2
            n = 1
            for x in shape[1:]:
                n *= x
            nb = (n * esz + 31) // 32 * 32
            assert self.off + nb <= ARENA, ("arena overflow", self.off, nb)
            ap = arena[:, self.off // 2:(self.off + n * esz) // 2]
            self.off += nb
            if dt != BF16:
                ap = ap.bitcast(dt)
            ap = ap[0:shape[0], :]
            if len(shape) == 3:
                ap = ap.rearrange("p (a b) -> p a b", a=shape[1])
            elif len(shape) == 4:
                ap = ap.rearrange("p (a b c) -> p a b c", a=shape[1], b=shape[2])
            return ap

    cv = Carver()
    Wg = cv.get([128, 8, DFF], BF16)
    Wu = cv.get([128, 8, DFF], BF16)
    Wd = cv.get([128, 22, D], BF16)
    wtmp = Carver(0)
    Win = wtmp.get([128, 8, DIN], BF16)
    wtmp = Carver(0)
    Wout = wtmp.get([128, 8, D], BF16)

    xs = cv.get([128, 4, D])
    gvec = cv.get([128, D])
    hb = [cv.get([128, D], BF16) for i in range(2)]
    hT = cv.get([128, 8, 512], BF16)
    actT = cv.get([128, 22, 512], BF16)
    sg = [cv.get([128, 512]) for i in range(2)]
    junk = cv.get([128, D], BF16)
    stage = cv.get([128, 4, 512])
    ssq = sb("ssq", [128, 8])
    rstd = sb("rstd", [128, 8])

    pT = ps("pT", [128, 8, 128], BF16)
    pA = [ps("pA%d" % i, [128, 512]) for i in range(2)]
    pB = [ps("pB%d" % i, [128, 512]) for i in range(2)]
    pO = [ps("pO%d" % i, [128, 512]) for i in range(2)]

    def load_w_cast(dst3, src2, nk, width, tok):
        for k in range(nk):
            c = 0
            while c < width:
                cw = min(1024, width - c)
                P.dma(dst3[:, k, c:c + cw], src2[k * 128:(k + 1) * 128, c:c + cw], w=[tok], q="pool")
                c += cw

    def load_group_x(r0, subs):
        for j, (o, rows) in enumerate(subs):
            P.dma(xs[:rows, j, :], X[r0 + o:r0 + o + rows, :], r=[("X", r0 + o)], w=[("xs", j)])

    def store_group_x(r0, subs):
        for j, (o, rows) in enumerate(subs):
            P.dma(X[r0 + o:r0 + o + rows, :], xs[:rows, j, :], r=[("xs", j)], w=[("X", r0 + o)], q="sp")

    def rstd_of_xs(subs):
        ns = len(subs)
        P.memset("dve", ssq[:, 0:ns], 0.0, w=["ssq"])
        for j, (o, rows) in enumerate(subs):
            P.act(junk[:rows, :], xs[:rows, j, :], AF.Square, r=[("xs", j), "ssq"], w=["junk", "ssq"],
                  accum_out=ssq[:rows, j:j + 1])
        P.act(rstd[:, 0:ns], ssq[:, 0:ns], AF.Sqrt, r=["ssq"], w=["rstd"], bias=epsc[:, 0:1], scale=1.0 / D)
        P.op("dve", lambda e: e.reciprocal(out=rstd[:, 0:ns], in_=rstd[:, 0:ns]), r=["rstd"], w=["rstd"])

    def norm_to_hT(subs, gain=True):
        rstd_of_xs(subs)
        for j, (o, rows) in enumerate(subs):
            h = hb[j % 2]
            P.stt("dve", h[:rows, :], xs[:rows, j, :], rstd[:rows, j:j + 1], gvec[:rows, :], ALU.mult, ALU.mult,
                  r=[("xs", j), "rstd", "gvec"], w=[("hb", j % 2)])
            for k in range(8):
                P.tr(pT[:, k, :rows], h[:rows, k * 128:(k + 1) * 128], ident_b[:rows, :rows],
                     r=[("hb", j % 2), "ident_b"], w=["pT"])
            P.copy("act", hT[:, :, o:o + rows], pT[:, :, :rows], r=["pT"], w=["hT"])

    def pass_init():
        P.dma(X[0:16, :], I["meta_tokens"][:, :], w=[("X", 0)])
        P.dma(X[16:L, :], I["x_prompt"][:, :], w=[("X", g[0] + o) for g in groups for (o, _) in g[2]])
        P.dma(X[L:LT, :], I["x_sample"][:, :], w=[("X", g[0] + o) for g in groups for (o, _) in g[2]])

    def pass_ffn(l, which):
        ln = I["ln_ffn%d" % which][l:l + 1, :]
        P.dma(gvec[:, :], ln.partition_broadcast(128) if False else ln.to_broadcast([128, D]), w=["gvec"])
        load_w_cast(Wg, I["ffn%d_gate" % which][l], 8, DFF, "Wg")
        load_w_cast(Wu, I["ffn%d_up" % which][l], 8, DFF, "Wu")
        load_w_cast(Wd, I["ffn%d_down" % which][l], 22, D, "Wd")
        for (r0, n, subs) in groups:
            load_group_x(r0, subs)
            norm_to_hT(subs)
            for f in range(22):
                a = pA[f % 2]
                b = pB[f % 2]
                for k in range(8):
                    P.mm(a[:, :n], Wg[:, k, f * 128:(f + 1) * 128], hT[:, k, :n], start=(k == 0), stop=(k == 7),
                         r=["Wg", "hT"], w=[("pA", f % 2)])
                for k in range(8):
                    P.mm(b[:, :n], Wu[:, k, f * 128:(f + 1) * 128], hT[:, k, :n], start=(k == 0), stop=(k == 7),
                         r=["Wu", "hT"], w=[("pB", f % 2)])
                s_ = sg[f % 2]
                P.act(s_[:, :n], a[:, :n], AF.Silu, r=[("pA", f % 2)], w=[("sg", f % 2)])
                P.tt("dve", actT[:, f, :n], s_[:, :n], b[:, :n], ALU.mult,
                     r=[("sg", f % 2), ("pB", f % 2)], w=[("actT", f)])
            cnt = 0
            for j, (o, rows) in enumerate(subs):
                for dh in range(2):
                    po = pO[cnt % 2]
                    for f in range(22):
                        P.mm(po[:rows, :], actT[:, f, o:o + rows], Wd[:, f, dh * 512:(dh + 1) * 512],
                             start=(f == 0), stop=(f == 21), r=[("actT", f), "Wd"], w=[("pO", cnt % 2)])
                    P.stt("dve", xs[:rows, j, dh * 512:(dh + 1) * 512], po[:rows, :], 0.5,
                          xs[:rows, j, dh * 512:(dh + 1) * 512], ALU.mult, ALU.add,
                          r=[("pO", cnt % 2), ("xs", j)], w=[("xs", j)])
                    cnt += 1
            store_group_x(r0, subs)

    T_CHUNKS = [(C_QA, 128), (C_KA, 128), (C_RA, 16), (C_QB, 128), (C_QB + 128, 128), (C_KB, 128), (C_KB + 128, 128),
                (C_FB, 4), (C_BC, 128), (C_BC + 128, 128), (C_CC, 128), (C_CC + 128, 128), (C_HC, 128),
                (C_HC + 128, 128), (C_GD, 128), (C_GD + 128, 128), (C_XD, 128), (C_XD + 128, 128)]
    R_CHUNKS = [(C_VA, 512), (C_KB, 512), (C_FB, 4)]

    def pass_in(l):
        ln = I["ln_mix"][l:l + 1, :]
        P.dma(gvec[:, :], ln.to_broadcast([128, D]), w=["gvec"])
        load_w_cast(Win, I["w_in"][l], 8, DIN, "Wg")
        P.op("pool", lambda e: e.memset(junk[0:1, 0:1], 0.0), r=[], w=["Wu", "Wd", "junk"])
        for (r0, n, subs) in groups:
            load_group_x(r0, subs)
            norm_to_hT(subs)
            ci = 0
            for (c0, cw) in T_CHUNKS:
                a = pA[ci % 2]
                for k in range(8):
                    P.mm(a[:cw, :n], Win[:, k, c0:c0 + cw], hT[:, k, :n], start=(k == 0), stop=(k == 7),
                         r=["Wg", "Wu", "Wd", "hT"], w=[("pA", ci % 2)])
                s_ = stage[:, ci % 4, :]
                P.copy("act" if ci % 2 == 0 else "dve", s_[:cw, :n], a[:cw, :n], r=[("pA", ci % 2)], w=[("stage", ci % 4)])
                P.dma(ZT[c0:c0 + cw, r0:r0 + n], s_[:cw, :n], r=[("stage", ci % 4)], w=[("ZT", r0)], q="sp")
                ci += 1
            for j, (o, rows) in enumerate(subs):
                for (c0, cw) in R_CHUNKS:
                    a = pA[ci % 2]
                    for k in range(8):
                        P.mm(a[:rows, :cw], hT[:, k, o:o + rows], Win[:, k, c0:c0 + cw], start=(k == 0), stop=(k == 7),
                             r=["Wg", "Wu", "Wd", "hT"], w=[("pA", ci % 2)])
                    s_ = stage[:, ci % 4, :]
                    P.copy("act" if ci % 2 == 0 else "dve", s_[:rows, :cw], a[:rows, :cw], r=[("pA", ci % 2)],
                           w=[("stage", ci % 4)])
                    P.dma(Z[r0 + o:r0 + o + rows, c0:c0 + cw], s_[:rows, :cw], r=[("stage", ci % 4)],
                          w=[("Z", r0)], q="sp")
                    ci += 1

    yts = stage
    ytb = hT
    gn = sb("gn", [128, 8])

    def pass_out(l):
        load_w_cast(Wout, I["w_out"][l], 8, D, "Wg")
        P.op("pool", lambda e: e.memset(junk[0:1, 0:1], 0.0), r=[], w=["Wu", "Wd", "junk"])
        P.dma(gn[:, :], I["g_norm"][l].rearrange("(k p) -> p k", p=128), w=["gn"], allow_slow_non_contiguous=True)
        for (r0, n, subs) in groups:
            load_group_x(r0, subs)
            for k in range(8):
                s_ = stage[:, k % 4, :]
                P.dma(s_[:, :n], YT[k * 128:(k + 1) * 128, r0:r0 + n], r=[("YT", r0)], w=[("stage", k % 4)])
                P.ts("dve" if k % 2 else "pool", ytb[:, k, :n], s_[:, :n], gn[:, k:k + 1], None, ALU.mult,
                     r=[("stage", k % 4), "gn"], w=["hT"])
            cnt = 0
            for j, (o, rows) in enumerate(subs):
                for dh in range(2):
                    po = pO[cnt % 2]
                    for k in range(8):
                        P.mm(po[:rows, :], ytb[:, k, o:o + rows], Wout[:, k, dh * 512:(dh + 1) * 512],
                             start=(k == 0), stop=(k == 7), r=["hT", "Wg", "Wu", "Wd"], w=[("pO", cnt % 2)])
                    P.tt("dve", xs[:rows, j, dh * 512:(dh + 1) * 512], po[:rows, :],
                         xs[:rows, j, dh * 512:(dh + 1) * 512], ALU.add,
                         r=[("pO", cnt % 2), ("xs", j)], w=[("xs", j)])
                    cnt += 1
            store_group_x(r0, subs)

    def pass_final():
        P.dma(gvec[:, :], I["ln_final"][0:1, :].to_broadcast([128, D]), w=["gvec"])
        for (r0, n, subs) in groups:
            load_group_x(r0, subs)
            rstd_of_xs(subs)
            for j, (o, rows) in enumerate(subs):
                P.stt("dve", xs[:rows, j, :], xs[:rows, j, :], rstd[:rows, j:j + 1], gvec[:rows, :], ALU.mult, ALU.mult,
                      r=[("xs", j), "rstd", "gvec"], w=[("xs", j)])
                a = r0 + o
                b = a + rows
                lo, hi = max(a, 16), min(b, L)
                if lo < hi:
                    P.dma(O["y_prompt"][lo - 16:hi - 16, :], xs[lo - a:hi - a, j, :], r=[("xs", j)], w=["y_prompt"], q="sp")
                lo, hi = max(a, L), min(b, LT)
                if lo < hi:
                    P.dma(O["y_sample"][lo - L:hi - L, :], xs[lo - a:hi - a, j, :], r=[("xs", j)], w=["y_sample"], q="sp")


    mask_qk = sb("mask_qk", [128, 128])
    tri_le = sb("tri_le", [128, 128])
    tri_gt = sb("tri_gt", [128, 128])
    onec = sb("onec", [128, 1])
    iota_f = sb("iota_f", [128, 1])
    iota_i = sb("iota_i", [128, 1], I32)
    P.memset("pool", onec[:], 1.0, w=["onec"])
    P.op("pool", lambda e: e.affine_select(out=mask_qk[:], in_=ones_f[:], pattern=[[1, 128]], compare_op=ALU.is_ge,
                                           fill=0.0, base=0, channel_multiplier=-1), r=["ones_f"], w=["mask_qk"])
    P.op("pool", lambda e: e.affine_select(out=tri_le[:], in_=ones_f[:], pattern=[[1, 128]], compare_op=ALU.is_ge,
                                           fill=0.0, base=0, channel_multiplier=-1), r=["ones_f"], w=["tri_le"])
    P.op("pool", lambda e: e.affine_select(out=tri_gt[:], in_=ones_f[:], pattern=[[-1, 128]], compare_op=ALU.is_gt,
                                           fill=0.0, base=0, channel_multiplier=1), r=["ones_f"], w=["tri_gt"])
    P.op("pool", lambda e: e.iota(iota_i[:], pattern=[[0, 1]], base=0, channel_multiplier=1), w=["iota_i"])
    P.copy("dve", iota_f[:], iota_i[:], r=["iota_i"], w=["iota_f"])

    hm2 = sb("hm2", [128, 2])
    P.memset("dve", hm2[:], 0.0, w=["hm2"])
    P.memset("dve", hm2[0:64, 0:1], 0.125, w=["hm2"])
    P.memset("dve", hm2[64:128, 1:2], 0.125, w=["hm2"])
    NTILE = (L + 127) // 128
    def tile_rows(t):
        return min(128, L - t * 128)

    def pass_mix(l):
        P.barrier()
        m = Carver(0)
        cw = m.get([128, 2, 3]); dw = m.get([128, 2, 4]); db = m.get([128, 2]); br = m.get([128, 2]); bi = m.get([128, 2])
        lam = m.get([128, 2]); coef = m.get([128, 2]); coef2 = m.get([128, 2])
        nba = m.get([128, 1]); bfr = m.get([128, 4]); wal = m.get([16, 128])
        BDr = [m.get([128, 128], BF16) for _ in range(2)]; BDi = [m.get([128, 128], BF16) for _ in range(2)]
        kw = dict(allow_slow_non_contiguous=True)
        for c in range(2):
            P.dma(cw[:, c, :], I["conv_c_w"][l][:, c * 128:(c + 1) * 128].rearrange("j p -> p j"), w=["prm"], **kw)
            P.dma(dw[:, c, :], I["conv_d_w"][l][:, c * 128:(c + 1) * 128].rearrange("j p -> p j"), w=["prm"], **kw)
        for t_, nm in ((db, "conv_d_b"), (br, "lru_b_r"), (bi, "lru_b_i"), (lam, "lru_lambda")):
            P.dma(t_, I[nm][l].rearrange("(c p) -> p c", p=128), w=["prm"], **kw)
        P.dma(nba, I["b_alpha"][l].rearrange("(p o) -> p o", o=1), w=["prm"], **kw)
        P.dma(bfr, I["b_forget"][l:l + 1, :].to_broadcast([128, 4]), w=["prm"])
        P.dma(wal, I["w_alpha_up"][l], w=["prm"])
        for c in range(2):
            for (BD, nm) in ((BDr, "lru_w_r"), (BDi, "lru_w_i")):
                P.memset("pool", BD[c], 0.0, w=[("BD", nm, c)])
                for hh in range(2):
                    P.dma(BD[c][hh * 64:(hh + 1) * 64, hh * 64:(hh + 1) * 64], I[nm][l, 2 * c + hh], w=[("BD", nm, c)], q="pool")
        P.act(coef, lam, AF.Exp, r=["prm"], w=["coef"], scale=-1.0)
        P.act(coef, coef, AF.Ln, r=["coef"], w=["coef"], bias=onec[:, 0:1])
        P.ts("dve", coef2, coef, -16.0, None, ALU.mult, r=["coef"], w=["coef2"])
        P.ts("dve", coef, coef, -8.0, None, ALU.mult, r=["coef", "coef2"], w=["coef"])
        P.ts("dve", nba, nba, -1.0, None, ALU.mult, r=["prm"], w=["nba"])

        prm_end = m.off
        NW = 520
        tl = {nm: m.get([128, 2, NW]) for nm in ("bc", "ta", "tb", "U", "acc", "gd", "XD", "xc", "R", "IG", "A", "UU", "H", "t1")}
        xcb = m.get([128, 2, 512], BF16)
        rs = m.get([128, 512])
        hcar = m.get([128, 2]); h0s = m.get([128, 2, max(NS, 1)])
        P.memset("dve", hcar, 0.0, w=["hcar"])
        psr = pA[0]; psi = pA[1]; pss = pB[0]

        def v3(t, c, nseq, w, a, b):
            return t[:, c, 0:nseq * w].rearrange("p (s w) -> p s w", s=nseq)[:, :, a:b]

        def rms_T_store(V, vtok, n, sq, sqtok):
            for c in range(2):
                P.tt("pool", sq[:, c, :n], V[:, c, :n], V[:, c, :n], ALU.mult, r=[vtok], w=[sqtok])
            for c in range(2):
                P.mm(pss[:, :n], ones_f[:, :], sq[:, c, :n], start=(c == 0), stop=(c == 1), r=[sqtok, "ones_f"], w=["pB0"])
            P.act(rs[:, :n], pss[:, :n], AF.Sqrt, r=["pB0"], w=["rs"], bias=epsc[:, 0:1], scale=1.0 / 256)
            P.op("dve", lambda e: e.reciprocal(out=rs[:, :n], in_=rs[:, :n]), r=["rs"], w=["rs"])
            for c in range(2):
                P.tt("dve", sq[:, c, :n], V[:, c, :n], rs[:, :n], ALU.mult, r=[vtok, "rs", sqtok], w=[sqtok])
            return sq

        def mix_cd(c0, nseq, T, mode, first, last):
            n = nseq * T
            U, XD = tl["U"], tl["XD"]
            if mode == "p":
                if first:
                    for c in range(2):
                        P.memset("pool", U[:, c, 0:2], 0.0, w=["U"])
                        P.memset("pool", XD[:, c, 0:3], 0.0, w=["XD"])
            else:
                for c in range(2):
                    for s_ in range(nseq):
                        P.dma(v3(U, c, nseq, 2 + T, 0, 2)[:, s_, :], I["state_conv_c"][l][s_, :, c * 128:(c + 1) * 128].rearrange("j p -> p j"),
                              w=["U"], **kw)
                        P.dma(v3(XD, c, nseq, 3 + T, 0, 3)[:, s_, :], I["state_conv_d"][l][s_, :, c * 128:(c + 1) * 128].rearrange("j p -> p j"),
                              w=["XD"], **kw)
                    P.dma(h0s[:, c, 0:nseq], I["state_rglru_h"][l][:, c * 128:(c + 1) * 128].rearrange("s p -> p s"), w=["h0s"], **kw)
            for c in range(2):
                P.dma(tl["bc"][:, c, :n], ZT[C_BC + c * 128:C_BC + (c + 1) * 128, c0:c0 + n], w=["bc"])
                P.dma(tl["ta"][:, c, :n], ZT[C_CC + c * 128:C_CC + (c + 1) * 128, c0:c0 + n], w=["ta"])
                P.dma(tl["tb"][:, c, :n], ZT[C_HC + c * 128:C_HC + (c + 1) * 128, c0:c0 + n], w=["tb"])
                P.dma(tl["gd"][:, c, :n], ZT[C_GD + c * 128:C_GD + (c + 1) * 128, c0:c0 + n], w=["gd"])
                P.dma(v3(XD, c, nseq, 3 + T, 3, 3 + T),
                      ZT[C_XD + c * 128:C_XD + (c + 1) * 128, c0:c0 + n].rearrange("p (s t) -> p s t", s=nseq), r=["XD"], w=["XD"])
            acc = tl["acc"]
            for c in range(2):
                P.tt("pool", v3(U, c, nseq, 2 + T, 2, 2 + T), tl["ta"][:, c, :n].rearrange("p (s t) -> p s t", s=nseq),
                     tl["tb"][:, c, :n].rearrange("p (s t) -> p s t", s=nseq), ALU.mult, r=["ta", "tb", "U"], w=["U"])
                a3 = acc[:, c, :n].rearrange("p (s t) -> p s t", s=nseq)
                P.ts("dve", a3, v3(U, c, nseq, 2 + T, 0, T), cw[:, c, 0:1], None, ALU.mult, r=["U", "prm"], w=["acc"])
                P.stt("dve", a3, v3(U, c, nseq, 2 + T, 1, 1 + T), cw[:, c, 1:2], a3, ALU.mult, ALU.add, r=["U", "acc"], w=["acc"])
                P.stt("dve", a3, v3(U, c, nseq, 2 + T, 2, 2 + T), cw[:, c, 2:3], a3, ALU.mult, ALU.add, r=["U", "acc"], w=["acc"])
                P.tt("dve", acc[:, c, :n], acc[:, c, :n], tl["bc"][:, c, :n], ALU.mult, r=["acc", "bc"], w=["acc"])
            y = rms_T_store(acc, "acc", n, tl["t1"], "t1")
            for c in range(2):
                P.dma(YT[512 + c * 128:512 + (c + 1) * 128, c0:c0 + n], y[:, c, :n], r=["t1"], w=["YTc"])
            if mode == "p":
                if last:
                    for c in range(2):
                        P.dma(O["conv_c_prompt"][l][:, c * 128:(c + 1) * 128].rearrange("j p -> p j"), U[:, c, T:T + 2],
                              r=["U"], w=["o_cc"], **kw)
                else:
                    for c in range(2):
                        P.copy("pool", tl["ta"][:, c, 0:2], U[:, c, T:T + 2], r=["U", "ta"], w=["ta"])
                        P.copy("pool", U[:, c, 0:2], tl["ta"][:, c, 0:2], r=["ta", "U"], w=["U"])
            else:
                for c in range(2):
                    for s_ in range(nseq):
                        P.dma(O["conv_c_sample"][l][s_, :, c * 128:(c + 1) * 128].rearrange("j p -> p j"),
                              v3(U, c, nseq, 2 + T, T, T + 2)[:, s_, :], r=["U"], w=["o_cc"], **kw)
            xc = tl["xc"]
            for c in range(2):
                x3 = xc[:, c, :n].rearrange("p (s t) -> p s t", s=nseq)
                P.ts("dve", x3, v3(XD, c, nseq, 3 + T, 0, T), dw[:, c, 0:1], db[:, c:c + 1], ALU.mult, ALU.add, r=["XD", "prm"], w=["xc"])
                for j in range(1, 4):
                    P.stt("dve", x3, v3(XD, c, nseq, 3 + T, j, j + T), dw[:, c, j:j + 1], x3, ALU.mult, ALU.add, r=["XD", "xc"], w=["xc"])
                P.copy("pool", xcb[:, c, :n], xc[:, c, :n], r=["xc"], w=["xcb"])
            if mode == "p":
                if last:
                    for c in range(2):
                        P.dma(O["conv_d_prompt"][l][:, c * 128:(c + 1) * 128].rearrange("j p -> p j"), XD[:, c, T:T + 3],
                              r=["XD"], w=["o_cd"], **kw)
                else:
                    for c in range(2):
                        P.copy("pool", tl["tb"][:, c, 0:3], XD[:, c, T:T + 3], r=["XD", "tb"], w=["tb"])
                        P.copy("pool", XD[:, c, 0:3], tl["tb"][:, c, 0:3], r=["tb", "XD"], w=["XD"])
            else:
                for c in range(2):
                    for s_ in range(nseq):
                        P.dma(O["conv_d_sample"][l][s_, :, c * 128:(c + 1) * 128].rearrange("j p -> p j"),
                              v3(XD, c, nseq, 3 + T, T, T + 3)[:, s_, :], r=["XD"], w=["o_cd"], **kw)
            R, IG, A_, UU, H, t1 = tl["R"], tl["IG"], tl["A"], tl["UU"], tl["H"], tl["t1"]
            for c in range(2):
                P.mm(psr[:, :n], BDr[c], xcb[:, c, :n], r=[("BD", "lru_w_r", c), "xcb"], w=["pA0"])
                P.mm(psi[:, :n], BDi[c], xcb[:, c, :n], r=[("BD", "lru_w_i", c), "xcb"], w=["pA1"])
                P.act(R[:, c, :n], psr[:, :n], AF.Sigmoid, r=["pA0", "prm"], w=["R"], bias=br[:, c:c + 1])
                P.act(IG[:, c, :n], psi[:, :n], AF.Sigmoid, r=["pA1", "prm"], w=["IG"], bias=bi[:, c:c + 1])
            for c in range(2):
                P.act(A_[:, c, :n], R[:, c, :n], AF.Exp, r=["R", "coef"], w=["A"], scale=coef[:, c:c + 1])
                P.act(t1[:, c, :n], R[:, c, :n], AF.Exp, r=["R", "coef2"], w=["t1"], scale=coef2[:, c:c + 1])
                P.ts("dve", t1[:, c, :n], t1[:, c, :n], -1.0, 1.0, ALU.mult, ALU.add, r=["t1"], w=["t1"])
            for c in range(2):
                P.act(t1[:, c, :n], t1[:, c, :n], AF.Sqrt, r=["t1"], w=["t1"])
                P.tt("dve", UU[:, c, :n], IG[:, c, :n], xc[:, c, :n], ALU.mult, r=["IG", "xc"], w=["UU"])
                P.tt("dve", UU[:, c, :n], UU[:, c, :n], t1[:, c, :n], ALU.mult, r=["UU", "t1"], w=["UU"])
            for c in range(2):
                if mode == "p":
                    P.scan("dve", H[:, c, :n], A_[:, c, :n], UU[:, c, :n], hcar[:, c:c + 1], r=["A", "UU", "hcar"], w=["H"])
                    P.copy("dve", hcar[:, c:c + 1], H[:, c, n - 1:n], r=["H", "hcar"], w=["hcar"])
                else:
                    a3 = A_[:, c, :n].rearrange("p (s t) -> p s t", s=nseq)
                    u3 = UU[:, c, :n].rearrange("p (s t) -> p s t", s=nseq)
                    P.tt("dve", t1[:, c, 0:nseq], a3[:, :, 0], h0s[:, c, 0:nseq], ALU.mult, r=["A", "h0s", "t1"], w=["t1"])
                    P.tt("dve", u3[:, :, 0], u3[:, :, 0], t1[:, c, 0:nseq], ALU.add, r=["UU", "t1"], w=["UU"])
                    P.memset("dve", a3[:, :, 0:1], 0.0, w=["A"])
                    P.scan("dve", H[:, c, :n], A_[:, c, :n], UU[:, c, :n], 0.0, r=["A", "UU"], w=["H"])
            if mode == "p":
                if last:
                    for c in range(2):
                        P.dma(O["rglru_h_prompt"][l][:, c * 128:(c + 1) * 128].rearrange("o p -> p o"), H[:, c, n - 1:n],
                              r=["H"], w=["o_h"], **kw)
            else:
                for c in range(2):
                    P.dma(O["rglru_h_sample"][l][:, c * 128:(c + 1) * 128].rearrange("s p -> p s"),
                          H[:, c, :n].rearrange("p (s t) -> p s t", s=nseq)[:, :, T - 1], r=["H"], w=["o_h"], **kw)
            gd = tl["gd"]
            for c in range(2):
                P.tt("pool", t1[:, c, :n], gd[:, c, :n], gd[:, c, :n], ALU.mult, r=["gd", "t1"], w=["t1"])
                P.ts("dve", t1[:, c, :n], t1[:, c, :n], 0.044715, 1.0, ALU.mult, ALU.add, r=["t1"], w=["t1"])
                P.tt("dve", t1[:, c, :n], t1[:, c, :n], gd[:, c, :n], ALU.mult, r=["t1", "gd"], w=["t1"])
                P.act(t1[:, c, :n], t1[:, c, :n], AF.Sigmoid, r=["t1"], w=["t1"], scale=1.5957691216057308)
                P.tt("dve", t1[:, c, :n], t1[:, c, :n], gd[:, c, :n], ALU.mult, r=["t1", "gd"], w=["t1"])
                P.tt("dve", H[:, c, :n], H[:, c, :n], t1[:, c, :n], ALU.mult, r=["H", "t1"], w=["H"])
            y = rms_T_store(H, "H", n, tl["R"], "R")
            for c in range(2):
                P.dma(YT[768 + c * 128:768 + (c + 1) * 128, c0:c0 + n], y[:, c, :n], r=["R"], w=["YTd"])

        SUB = cfg.get("sub", "cd,gla,fox,foxs").split(",")
        nblk = (L + 511) // 512
        for b_ in range(nblk if "cd" in SUB else 0):
            c0 = b_ * 512
            T = min(512, L - c0)
            mix_cd(c0, 1, T, "p", b_ == 0, b_ == nblk - 1)
        if "cd" in SUB:
            mix_cd(L, NS, 4, "s", True, True)

        PARTS = [(0, 96), (96, 32)]
        HP = [(0, 0), (0, 32), (0, 64), (1, 0)]
        def two(shape, dt=F32):
            return [m.get(shape, dt) for _ in range(2)]
        QT = two([128, 128]); KT = two([128, 128]); RAT = m.get([16, 128]); VG = m.get([128, 512])
        LA = two([128, 128]); Bc = two([128, 128]); EB = two([128, 128]); KE = two([128, 128])
        QTs = two([128, 128], BF16); KTs = two([128, 128], BF16); KENDR = m.get([128, 128], BF16)
        VB = m.get([128, 256], BF16); SCM = m.get([128, 4, 128], BF16)
        S = two([128, 64]); Sb = two([128, 64], BF16); EBL = two([128, 1]); nbap = two([128, 1])
        OS = m.get([128, 4, 64]); SQA = m.get([128, 4, 64]); SSA = m.get([128, 4]); SGA = m.get([128, 256]); YS = m.get([128, 2, 128])
        QTm = [m.get([128, 128], BF16) for _ in range(3)]
        hm = m.get([128, 3])
        P.memset("dve", hm, 0.0, w=["hm"])
        for h in range(3):
            P.memset("dve", hm[32 * h:32 * h + 32, h:h + 1], 1.0, w=["hm"])
        psc = pA[0].rearrange("p (h q) -> p h q", h=4)
        pso = pA[1][:, 0:256].rearrange("p (h d) -> p h d", h=4)
        psS0 = pA[1][:, 256:448]
        psS1 = pB[1][:, 256:320]
        psl = [pB[0][:, 0:128], pB[0][:, 128:256]]
        pst = pB[0][:, 256:384]
        pyt = pB[1][:, 0:256].rearrange("p (c q) -> p c q", c=2)
        for pi, (r0, nr) in enumerate(PARTS):
            P.dma(nbap[pi][:nr, :], I["b_alpha"][l][r0:r0 + nr].rearrange("(p o) -> p o", o=1), w=["nbap"], **kw)
            P.ts("dve", nbap[pi][:nr, :], nbap[pi][:nr, :], -1.0, None, ALU.mult, r=["nbap"], w=["nbap"])

        GSTOP = cfg.get("gstop", 9)

        def gla_chunk(c0, n):
            P.dma(RAT[:, :n], ZT[C_RA:C_RA + 16, c0:c0 + n], w=["RAT"])
            P.dma(VG[:n, :], Z[c0:c0 + n, C_VA:C_VA + 512], w=["VG"])
            for pi, (r0, nr) in enumerate(PARTS):
                P.dma(QT[pi][:nr, :n], ZT[C_QA + r0:C_QA + r0 + nr, c0:c0 + n], w=["QT"])
                P.dma(KT[pi][:nr, :n], ZT[C_KA + r0:C_KA + r0 + nr, c0:c0 + n], w=["KT"])
            for pi, (r0, nr) in enumerate(PARTS):
                P.mm(psl[pi][:nr, :n], wal[:, r0:r0 + nr], RAT[:, :n], r=["prm", "RAT"], w=["pB0"])
                la, bc, eb, ke = LA[pi][:nr, :n], Bc[pi][:nr, :n], EB[pi][:nr, :n], KE[pi][:nr, :n]
                P.act(la, psl[pi][:nr, :n], AF.Exp, r=["pB0", "nbap"], w=["LA"], scale=-1.0, bias=nbap[pi][:nr, 0:1])
                P.act(la, la, AF.Ln, r=["LA"], w=["LA"], bias=onec[:nr, 0:1])
                P.ts("dve", la, la, -1.0 / 16.0, None, ALU.mult, r=["LA"], w=["LA"])
                P.scan("dve", bc, ones_f[:nr, :n], la, 0.0, r=["LA", "ones_f"], w=["Bc"])
                P.act(eb, bc, AF.Exp, r=["Bc"], w=["EB"])
                P.stt("dve", QTs[pi][:nr, :n], QT[pi][:nr, :n], 32 ** -0.5, eb, ALU.mult, ALU.mult, r=["QT", "EB"], w=["QTs"])
                P.act(eb, bc, AF.Exp, r=["Bc", "QTs"], w=["EB"], scale=-1.0)
                P.tt("dve", KTs[pi][:nr, :n], KT[pi][:nr, :n], eb, ALU.mult, r=["KT", "EB"], w=["KTs"])
                P.act(ke, bc, AF.Exp, r=["Bc"], w=["KE"], scale=-1.0, bias=Bc[pi][:nr, n - 1:n])
                P.tt("dve", ke, ke, KT[pi][:nr, :n], ALU.mult, r=["KE", "KT"], w=["KE"])
                P.act(EBL[pi][:nr, 0:1], Bc[pi][:nr, n - 1:n], AF.Exp, r=["Bc"], w=["EBL"])
                P.tr(pst[:n, r0:r0 + nr], ke, ident_f[:nr, :nr], r=["KE", "ident_f"], w=["pB0"])
            if GSTOP <= 0:
                return
            P.copy("act", KENDR[:n, :], pst[:n, :], r=["pB0"], w=["KENDR"])
            P.copy("pool", VB[:n, :], VG[:n, 0:256], r=["VG"], w=["VB"])
            if GSTOP <= 1:
                return
            for h in range(3):
                P.ts("pool" if h == 1 else "dve", QTm[h][:96, :n], QTs[0][:96, :n], hm[:96, h:h + 1], None, ALU.mult,
                     r=["QTs", "hm"], w=["QTm"])
            for h in range(4):
                if h < 3:
                    P.mm(psc[:n, h, :n], KTs[0][0:96, :n], QTm[h][0:96, :n], r=["KTs", "QTm"], w=["pA0"])
                else:
                    P.mm(psc[:n, h, :n], KTs[1][0:32, :n], QTs[1][0:32, :n], r=["KTs", "QTs"], w=["pA0"])
            for h in range(4):
                P.tt("dve", SCM[:n, h, :n], psc[:n, h, :n], mask_qk[:n, :n], ALU.mult, r=["pA0", "mask_qk"], w=["SCM"])
            if GSTOP <= 2:
                return
            for h in range(4):
                pi, pb = HP[h]
                P.mm(pso[:n, h, :], SCM[:n, h, :n], VB[:n, 64 * h:64 * h + 64], start=True, stop=False, r=["SCM", "VB"], w=["pA1"])
                if h < 3:
                    P.mm(pso[:n, h, :], QTm[h][0:96, :n], Sb[0][0:96, :], start=False, stop=True, r=["QTm", "Sb"], w=["pA1"])
                else:
                    P.mm(pso[:n, h, :], QTs[1][0:32, :n], Sb[1][0:32, :], start=False, stop=True, r=["QTs", "Sb"], w=["pA1"])
            if GSTOP <= 3:
                return
            P.mm(psS0[0:96, :], KENDR[:n, 0:96], VB[:n, 0:192], r=["KENDR", "VB"], w=["pA1"])
            P.mm(psS1[0:32, :], KENDR[:n, 96:128], VB[:n, 192:256], r=["KENDR", "VB"], w=["pB1"])
            for h in range(4):
                pi, pb = HP[h]
                src = psS0[pb:pb + 32, 64 * h:64 * h + 64] if pi == 0 else psS1[0:32, :]
                P.stt("dve", S[pi][pb:pb + 32, :], S[pi][pb:pb + 32, :], EBL[pi][pb:pb + 32, 0:1], src, ALU.mult, ALU.add,
                      r=["S", "EBL", "pA1", "pB1"], w=["S"])
            for pi, (r0, nr) in enumerate(PARTS):
                P.copy("pool", Sb[pi][:nr, :], S[pi][:nr, :], r=["S"], w=["Sb"])
            if GSTOP <= 4:
                return
            P.copy("act", OS[:n], pso[:n], r=["pA1"], w=["OS"])
            P.tt("pool", SQA[:n], OS[:n], OS[:n], ALU.mult, r=["OS"], w=["SQA"])
            P.op("dve", lambda e: e.tensor_reduce(out=SSA[:n, :], in_=SQA[:n], axis=AX.X, op=ALU.add), r=["SQA"], w=["SSA"])
            P.act(SSA[:n, :], SSA[:n, :], AF.Sqrt, r=["SSA"], w=["SSA"], bias=epsc[:n, 0:1], scale=1.0 / 64)
            P.op("dve", lambda e: e.reciprocal(out=SSA[:n, :], in_=SSA[:n, :]), r=["SSA"], w=["SSA"])
            P.act(SGA[:n, :], VG[:n, 256:512], AF.Silu, r=["VG"], w=["SGA"])
            for h in range(4):
                P.stt("dve", OS[:n, h, :], OS[:n, h, :], SSA[:n, h:h + 1], SGA[:n, 64 * h:64 * h + 64], ALU.mult, ALU.mult,
                      r=["OS", "SSA", "SGA"], w=["OS"])
            for c in range(2):
                P.tr(pyt[:, c, :n], OS[:n, 2 * c:2 * c + 2, :].rearrange("p h d -> p (h d)"), ident_f[:n, :n],
                     r=["OS", "ident_f"], w=["pB1"])
            P.copy("act", YS[:, :, :n], pyt[:, :, :n], r=["pB1"], w=["YS"])
            for c in range(2):
                P.dma(YT[c * 128:(c + 1) * 128, c0:c0 + n], YS[:, c, :n], r=["YS"], w=["YTa"])

        for pi, (r0, nr) in enumerate(PARTS):
            P.memset("dve", S[pi][:, :], 0.0, w=["S"])
            P.memset("pool", Sb[pi][:, :], 0.0, w=["Sb"])
        for t in range(NTILE if "gla" in SUB else 0):
            gla_chunk(t * 128, tile_rows(t))
        for pi, (r0, nr) in enumerate(PARTS):
            P.dma(O["gla_prompt"][l][r0:r0 + nr, :], S[pi][:nr, :], r=["S"], w=["o_gla"])
        for b_ in range(NS if "gla" in SUB else 0):
            for pi, (r0, nr) in enumerate(PARTS):
                P.dma(S[pi][:nr, :], I["state_gla"][l, b_][r0:r0 + nr, :], w=["S"])
                P.copy("pool", Sb[pi][:nr, :], S[pi][:nr, :], r=["S"], w=["Sb"])
            gla_chunk(L + 4 * b_, 4)
            for pi, (r0, nr) in enumerate(PARTS):
                P.dma(O["gla_sample"][l, b_][r0:r0 + nr, :], S[pi][:nr, :], r=["S"], w=["o_gla"])

        mb = Carver(m.off)
        NTS = NTILE + 1
        KT2 = mb.get([128, 2, NTILE * 128], BF16); QT2 = mb.get([128, 2, 128], BF16)
        VA = mb.get([128, NTILE, 4, 65], BF16)
        LFR = mb.get([128, NTS, 4]); CW = mb.get([128, NTS, 4]); TOT = mb.get([128, NTS, 4]); INC = mb.get([128, NTS, 4])
        TOTh = mb.get([128, 4, NTS]); INCh = mb.get([128, 4, NTS]); Ch = mb.get([128, 4, NTS])
        BIAS = mb.get([128, NTILE]); stg = mb.get([128, 2, 512]); vst = mb.get([128, 256])
        PTb = [mb.get([128, 128], BF16) for _ in range(2)]
        QT2m = [mb.get([128, 128], BF16) for _ in range(4)]
        Ep = [mb.get([128, 128]) for _ in range(2)]
        OB = mb.get([128, 4, 64]); RB = mb.get([128, 4]); SQB = mb.get([128, 256]); YSB = mb.get([128, 2, 128])
        pss_ = [pO[0], pO[1]]
        pob = pA[0][:, 0:65]
        pcw = pB[0]; ptot = pB[1]

        P.memset("dve", LFR, 0.0, w=["LFR"])
        for t in range(NTILE):
            P.dma(LFR[:tile_rows(t), t, :], Z[t * 128:t * 128 + tile_rows(t), C_FB:C_FB + 4], r=["LFR"], w=["LFR"], **kw)
        P.dma(LFR[:NT, NTILE, :], Z[L:LT, C_FB:C_FB + 4], r=["LFR"], w=["LFR"], **kw)
        for h in range(4):
            P.ts("dve", CW[:, :, h], LFR[:, :, h], bfr[:, h:h + 1], None, ALU.add, r=["LFR", "prm"], w=["CW"])
        P.act(CW, CW, AF.Exp, r=["CW"], w=["CW"], scale=-1.0)
        P.act(CW, CW, AF.Ln, r=["CW"], w=["CW"], bias=onec[:, 0:1])
        P.memset("dve", LFR, 0.0, w=["LFR"])
        for t in range(NTILE):
            P.ts("dve", LFR[:tile_rows(t), t, :], CW[:tile_rows(t), t, :], -1.0, None, ALU.mult, r=["CW", "LFR"], w=["LFR"])
        P.ts("dve", LFR[:NT, NTILE, :], CW[:NT, NTILE, :], -1.0, None, ALU.mult, r=["CW", "LFR"], w=["LFR"])
        for t in range(NTILE):
            P.dma(O["logf_prompt"][l][t * 128:t * 128 + tile_rows(t), :], LFR[:tile_rows(t), t, :], r=["LFR"], w=["o_lf"], **kw)
        P.dma(O["logf_sample"][l], LFR[:NT, NTILE, :], r=["LFR"], w=["o_lfs"], **kw)
        ncol = NTILE * 4
        LFf = LFR[:, 0:NTILE, :].rearrange("p t h -> p (t h)")
        P.mm(pcw[:, :ncol], tri_le[:, :], LFf, r=["tri_le", "LFR"], w=["pB0"])
        P.mm(ptot[:, :ncol], ones_f[:, :], LFf, r=["ones_f", "LFR"], w=["pB1"])
        pcw3 = pcw[:, :ncol].rearrange("p (t h) -> p t h", h=4)
        ptot3 = ptot[:, :ncol].rearrange("p (t h) -> p t h", h=4)
        P.copy("act", TOTh[:, :, 0:NTILE].rearrange("p h t -> p t h"), ptot3, r=["pB1"], w=["TOT"])
        for h in range(4):
            P.scan("dve", INCh[:, h, 0:NTILE], ones_f[:, 0:NTILE], TOTh[:, h, 0:NTILE], 0.0, r=["TOT", "ones_f"], w=["INC"])
        P.tt("dve", Ch[:, :, 0:NTILE].rearrange("p h t -> p t h"), pcw3, INCh[:, :, 0:NTILE].rearrange("p h t -> p t h"), ALU.add,
             r=["pB0", "INC"], w=["Ch"])
        P.tt("dve", Ch[:, :, 0:NTILE], Ch[:, :, 0:NTILE], TOTh[:, :, 0:NTILE], ALU.subtract, r=["Ch", "TOT"], w=["Ch"])

        P.memset("pool", VA[:, :, :, 64:65], 1.0, w=["VA"])
        for t0 in range(0, L, 512):
            n = min(512, L - t0)
            for c in range(2):
                P.dma(stg[:, c, :n], ZT[C_KB + c * 128:C_KB + (c + 1) * 128, t0:t0 + n], w=["stg"])
            for c in range(2):
                P.copy("dve", KT2[:, c, t0:t0 + n], stg[:, c, :n], r=["stg"], w=["KT2"])
        for t in range(NTILE):
            rws = tile_rows(t)
            P.dma(vst[:rws, :], Z[t * 128:t * 128 + rws, C_VB:C_VB + 256], w=["vst"])
            P.copy("pool", VA[:rws, t, :, 0:64], vst[:rws, :].rearrange("p (h d) -> p h d", h=4), r=["vst", "VA"], w=["VA"])
        P.dma(O["k_prompt"][l], Z[0:L, C_KB:C_KB + 256], w=["o_k"])
        P.dma(O["v_prompt"][l], Z[0:L, C_VB:C_VB + 256], w=["o_v"])
        P.dma(O["k_sample"][l], Z[L:LT, C_KB:C_KB + 256], w=["o_k"])
        P.dma(O["v_sample"][l], Z[L:LT, C_VB:C_VB + 256], w=["o_v"])

        unit = 0
        for i in range(NTILE if "fox" in SUB else 0):
            rq = tile_rows(i)
            for c in range(2):
                P.dma(stg[:, c, :rq], ZT[C_QB + c * 128:C_QB + (c + 1) * 128, i * 128:i * 128 + rq], w=["stg"])
            for h in range(4):
                P.ts("dve", QT2m[h][:, :rq], stg[:, h // 2, :rq], hm2[:, h % 2:h % 2 + 1], None, ALU.mult,
                     r=["stg", "hm2"], w=["QT2"])
            for h in range(4):
                hp, hc_ = 64 * (h % 2), h // 2
                P.ts("dve", BIAS[:, 0:i + 1], Ch[:, h, 0:i + 1], -1.0, INCh[:, h, i:i + 1], ALU.mult, ALU.add,
                     r=["Ch", "INC"], w=["BIAS"])
                pend_pv = None
                for j in range(i + 1):
                    rk = tile_rows(j)
                    pp = pss_[unit % 2]
                    pt = PTb[unit % 2]
                    ptok = "pO%d" % (unit % 2)
                    P.mm(pp[:rk, :rq], KT2[:, hc_, j * 128:j * 128 + rk], QT2m[h][:, :rq],
                         r=["KT2", "QT2"], w=[ptok])
                    P.act(pt[:rk, :rq], pp[:rk, :rq], AF.Exp, r=[ptok, "BIAS"], w=[("PTb", unit % 2)],
                          bias=BIAS[:rk, j:j + 1])
                    if j == i:
                        P.tt("dve", pt[:rk, :rq], pt[:rk, :rq], mask_qk[:rk, :rq], ALU.mult,
                             r=[("PTb", unit % 2), "mask_qk"], w=[("PTb", unit % 2)])
                    def pv(j=j, rk=rk, pt=pt, sl=unit % 2):
                        P.mm(pob[:rq, :], pt[:rk, :rq], VA[:rk, j, h, :], start=(j == 0), stop=(j == i),
                             r=[("PTb", sl), "VA"], w=["pA0"])
                    if pend_pv is not None:
                        pend_pv()
                    pend_pv = pv
                    unit += 1
                pend_pv()
                pend_pv = None
                P.op("dve", lambda e, rq=rq, h=h: e.reciprocal(out=RB[:rq, h:h + 1], in_=pob[:rq, 64:65]), r=["pA0"], w=["RB"])
                P.ts("dve", OB[:rq, h, :], pob[:rq, 0:64], RB[:rq, h:h + 1], None, ALU.mult, r=["pA0", "RB"], w=["OB"])
            fox_finish(OB, rq, i * 128, SQB, RB, YSB)

        P.barrier()
        if "foxs" in SUB:
            fox_sample(l, Carver(prm_end), CW, INC, LFR, NTILE)
        P.barrier()

    def fox_finish(OB, rq, c0, SQB, RB, YSB):
        pyt = pB[1][:, 0:256].rearrange("p (c q) -> p c q", c=2)
        OBf = OB[:rq].rearrange("p h d -> p (h d)")
        P.memset("dve", RB[:rq, 0:1], 0.0, w=["RB"])
        P.act(SQB[:rq, :], OBf, AF.Square, r=["OB", "RB"], w=["SQB", "RB"], accum_out=RB[:rq, 0:1])
        P.act(RB[:rq, 0:1], RB[:rq, 0:1], AF.Sqrt, r=["RB"], w=["RB"], bias=epsc[:rq, 0:1], scale=1.0 / 256)
        P.op("dve", lambda e: e.reciprocal(out=RB[:rq, 0:1], in_=RB[:rq, 0:1]), r=["RB"], w=["RB"])
        P.ts("dve", SQB[:rq, :], OBf, RB[:rq, 0:1], None, ALU.mult, r=["OB", "RB", "SQB"], w=["SQB"])
        for c in range(2):
            P.tr(pyt[:, c, :rq], SQB[:rq, c * 128:(c + 1) * 128], ident_f[:rq, :rq], r=["SQB", "ident_f"], w=["pB1"])
        P.copy("act", YSB[:, :, :rq], pyt[:, :, :rq], r=["pB1"], w=["YSB"])
        for c in range(2):
            P.dma(YT[256 + c * 128:256 + (c + 1) * 128, c0:c0 + rq], YSB[:, c, :rq], r=["YSB"], w=["YTb"])

    def fox_sample(l, mb, CW, INC, LFR, NTILE):
        kw = dict(allow_slow_non_contiguous=True)
        NPGS = NPG + 1
        KP = mb.get([128, NPG, 256]); VP = mb.get([128, NPG, 256])
        KTs = mb.get([128, 2, NPGS, 128], BF16); QTsm = mb.get([128, 4, 4], BF16); qst = mb.get([128, 2, 4])
        VAs = mb.get([128, NPGS, 4, 65], BF16); vn = mb.get([4, 256])
        nrow = NS * NPG
        nhalf = (nrow + 127) // 128
        LFP = mb.get([128, nhalf, 512]); PTc = mb.get([128, nhalf], I32)
        PTI = mb.get([128, nrow], I32); IDX = mb.get([128, nrow], I32)
        LFs = mb.get([128, NS, NPGS, 4]); SFX = mb.get([128, NS, NPGS, 4]); TOs = mb.get([128, NS, NPGS, 4]); OSF = mb.get([128, NS, NPGS, 4])
        Es = mb.get([128, NPGS, 4]); PTs = mb.get([128, NPGS, 4], BF16)
        OBs = mb.get([4, 4, 64]); RBs = mb.get([4, 4]); SQs = mb.get([4, 256]); YSs = mb.get([128, 2, 4])
        ptr = pB[0]; pq = pO[0][:, 0:NPGS * 4].rearrange("p (j q) -> p j q", q=4); pos = pO[1][:, 0:65]
        P.dma(PTI[:, :], I["page_table"][0:1, :].to_broadcast([128, nrow]), w=["PTI"])
        P.ts("dve", IDX[:, :], PTI[:, :], 128.0, iota_f[:, 0:1], ALU.mult, ALU.add, r=["PTI", "iota_f"], w=["IDX"])
        for hf in range(nhalf):
            nr = min(128, nrow - hf * 128)
            P.dma(PTc[:nr, hf:hf + 1], I["page_table"][0:1, hf * 128:hf * 128 + nr].rearrange("o p -> p o"), w=["PTc"], **kw)
            P.op("pool", lambda e, hf=hf, nr=nr: e.indirect_dma_start(
                out=LFP[:nr, hf, :], out_offset=None, in_=I["cache_logf"].rearrange("l n w -> (l n) w"),
                in_offset=bass.IndirectOffsetOnAxis(ap=PTc[:nr, hf:hf + 1], axis=0),
                element_offset=l * NPOOL * 512), r=["PTc"], w=["LFP"], dma=True)
        P.memset("dve", LFs, 0.0, w=["LFs"])
        for hf in range(nhalf):
            nr = min(128, nrow - hf * 128)
            nb = nr // NPG
            for h in range(4):
                P.tr(ptr[:, h * 128:h * 128 + nr], LFP[:nr, hf, :].rearrange("p (s h) -> p s h", h=4)[:, :, h], ident_f[:nr, :nr],
                     r=["LFP", "ident_f"], w=["pB0"])
            for h in range(4):
                P.copy("dve", LFs[:, hf * (128 // NPG):hf * (128 // NPG) + nb, 0:NPG, h],
                       ptr[:, h * 128:h * 128 + nr].rearrange("p (b j) -> p b j", j=NPG), r=["pB0", "LFs"], w=["LFs"])
        P.dma(LFs[0:4, :, NPG, :], O["logf_sample"][l].rearrange("(b t) h -> t b h", t=4), r=["o_lfs", "LFs"], w=["LFs"], **kw)
        ncol = NS * NPGS * 4
        LFsf = LFs.rearrange("p b j h -> p (b j h)")
        SFXf = SFX.rearrange("p b j h -> p (b j h)")
        TOsf = TOs.rearrange("p b j h -> p (b j h)")
        c = 0
        while c < ncol:
            w_ = min(512, ncol - c)
            P.mm(pB[0][:, :w_], tri_gt[:, :], LFsf[:, c:c + w_], r=["tri_gt", "LFs"], w=["pB0"])
            P.copy("act", SFXf[:, c:c + w_], pB[0][:, :w_], r=["pB0"], w=["SFX"])
            P.mm(pB[1][:, :w_], ones_f[:, :], LFsf[:, c:c + w_], r=["ones_f", "LFs"], w=["pB1"])
            P.copy("act", TOsf[:, c:c + w_], pB[1][:, :w_], r=["pB1"], w=["TOs"])
            c += w_
        P.memset("dve", OSF[:, :, NPG, :], 0.0, w=["OSF"])
        for j in range(NPG - 1, -1, -1):
            P.tt("dve", OSF[:, :, j, :], OSF[:, :, j + 1, :], TOs[:, :, j + 1, :], ALU.add, r=["OSF", "TOs"], w=["OSF"])
        P.tt("dve", SFX, SFX, OSF, ALU.add, r=["SFX", "OSF"], w=["SFX"])
        P.memset("pool", VAs[:, :, :, 64:65], 1.0, w=["VAs"])
        for b in range(NS):
            tb = L + 4 * b
            for j in range(NPG):
                col = b * NPG + j
                P.op("pool", lambda e, j=j, col=col: e.indirect_dma_start(
                    out=KP[:, j, :], out_offset=None, in_=I["cache_k"].rearrange("l n w -> (l n) w"),
                    in_offset=bass.IndirectOffsetOnAxis(ap=IDX[:, col:col + 1], axis=0),
                    element_offset=l * NPOOL * 128 * 256), r=["IDX"], w=["KP"], dma=True)
                P.op("pool", lambda e, j=j, col=col: e.indirect_dma_start(
                    out=VP[:, j, :], out_offset=None, in_=I["cache_v"].rearrange("l n w -> (l n) w"),
                    in_offset=bass.IndirectOffsetOnAxis(ap=IDX[:, col:col + 1], axis=0),
                    element_offset=l * NPOOL * 128 * 256), r=["IDX"], w=["VP"], dma=True)
            for c_ in range(2):
                for j0 in range(0, NPG, 4):
                    for jj in range(4):
                        P.tr(ptr[:, jj * 128:(jj + 1) * 128], KP[:, j0 + jj, c_ * 128:(c_ + 1) * 128], ident_f[:, :],
                             r=["KP", "ident_f"], w=["pB0"])
                    P.copy("act" if (j0 // 4) % 2 else "dve", KTs[:, c_, j0:j0 + 4, :],
                           ptr[:, 0:512].rearrange("p (j s) -> p j s", j=4), r=["pB0"], w=["KTs"])
                P.dma(KTs[:, c_, NPG, 0:4], ZT[C_KB + c_ * 128:C_KB + (c_ + 1) * 128, tb:tb + 4], r=["KTs"], w=["KTs"], q="pool")
                P.dma(qst[:, c_, :], ZT[C_QB + c_ * 128:C_QB + (c_ + 1) * 128, tb:tb + 4], w=["qst"])
            for h in range(4):
                P.ts("dve", QTsm[:, h, :], qst[:, h // 2, :], hm2[:, h % 2:h % 2 + 1], None, ALU.mult, r=["qst", "hm2"], w=["QTs"])
            P.copy("pool", VAs[:, 0:NPG, :, 0:64], VP.rearrange("p j (h d) -> p j h d", h=4), r=["VP", "VAs"], w=["VAs"])
            P.dma(vn[:, :], Z[tb:tb + 4, C_VB:C_VB + 256], w=["vn"])
            P.copy("pool", VAs[0:4, NPG, :, 0:64], vn.rearrange("p (h d) -> p h d", h=4), r=["vn", "VAs"], w=["VAs"])
            for h in range(4):
                hp, hc_ = 64 * (h % 2), h // 2
                for j in range(NPGS):
                    rk = 128 if j < NPG else 4
                    P.mm(pq[:rk, j, :], KTs[:, hc_, j, :rk], QTsm[:, h, :], r=["KTs", "QTs"], w=["pO0"])
                P.tt("dve", Es[:, 0:NPG, :], pq[:, 0:NPG, :], SFX[:, b, 0:NPG, h:h + 1].to_broadcast([128, NPG, 4]), ALU.add,
                     r=["pO0", "SFX"], w=["Es"])
                P.tt("dve", Es[0:4, NPG, :], pq[0:4, NPG, :], SFX[0:4, b, NPG, h:h + 1].to_broadcast([4, 4]), ALU.add,
                     r=["pO0", "SFX", "Es"], w=["Es"])
                P.act(PTs[:, 0:NPG, :], Es[:, 0:NPG, :], AF.Exp, r=["Es"], w=["PTs"])
                P.act(PTs[0:4, NPG, :], Es[0:4, NPG, :], AF.Exp, r=["Es", "PTs"], w=["PTs"])
                P.op("pool", lambda e: e.affine_select(out=PTs[0:4, NPG, :], in_=PTs[0:4, NPG, :], pattern=[[1, 4]],
                                                       compare_op=ALU.is_ge, fill=0.0, base=0, channel_multiplier=-1),
                     r=["PTs"], w=["PTs"])
                for j in range(NPGS):
                    rk = 128 if j < NPG else 4
                    P.mm(pos[0:4, :], PTs[:rk, j, :], VAs[:rk, j, h, :], start=(j == 0), stop=(j == NPGS - 1),
                         r=["PTs", "VAs"], w=["pO1"])
                P.op("dve", lambda e, h=h: e.reciprocal(out=RBs[0:4, h:h + 1], in_=pos[0:4, 64:65]), r=["pO1"], w=["RBs"])
                P.ts("dve", OBs[0:4, h, :], pos[0:4, 0:64], RBs[0:4, h:h + 1], None, ALU.mult, r=["pO1", "RBs"], w=["OB"])
            fox_finish(OBs, 4, tb, SQs, RBs, YSs)

    def pass_mix_dummy(l):
        z = stage[:, 0, :]
        P.memset("pool", z, 0.0, w=[("stage", 0)])
        for (r0, n, subs) in groups:
            for k in range(8):
                P.dma(YT[k * 128:(k + 1) * 128, r0:r0 + n], z[:, :n], r=[("stage", 0)], w=[("YT", r0)])

    pass_init()
    for l in range(DEPTH):
        pass_ffn(l, 1)
        pass_in(l)
        if cfg.get("mix", "full") == "dummy":
            pass_mix_dummy(l)
        else:
            pass_mix(l)
        pass_out(l)
        pass_ffn(l, 2)
    pass_final()

    P.emit(st)
    st.close()
    return nc


def _core_inputs(inp, prompt_idx, s0, NS, DEPTH):
    f = lambda a: np.ascontiguousarray(np.asarray(a))
    npool = inp["cache_k"].shape[1]
    d = {}
    d["x_prompt"] = f(inp["x_prompt"][prompt_idx])
    d["x_sample"] = f(np.asarray(inp["x_sample"])[s0:s0 + NS].reshape(NS * 4, D))
    d["meta_tokens"] = f(inp["meta_tokens"])
    d["cache_k"] = f(np.asarray(inp["cache_k"]).reshape(DEPTH, npool * 128, 256))
    d["cache_v"] = f(np.asarray(inp["cache_v"]).reshape(DEPTH, npool * 128, 256))
    d["cache_logf"] = f(np.asarray(inp["cache_logf"]).reshape(DEPTH, npool, 512))
    d["state_gla"] = f(np.asarray(inp["state_gla"])[:, s0:s0 + NS].reshape(DEPTH, NS, 128, 64))
    d["state_conv_c"] = f(np.asarray(inp["state_conv_c"])[:, s0:s0 + NS])
    d["state_rglru_h"] = f(np.asarray(inp["state_rglru_h"])[:, s0:s0 + NS])
    d["state_conv_d"] = f(np.asarray(inp["state_conv_d"])[:, s0:s0 + NS])
    d["page_table"] = f(np.asarray(inp["page_table"])[s0:s0 + NS].reshape(1, -1).astype(np.int32))
    for nm in ("ln_ffn1", "ln_mix", "g_norm", "ln_ffn2", "ffn1_gate", "ffn1_up", "ffn2_gate", "ffn2_up", "ffn1_down",
               "ffn2_down", "w_in", "w_out", "w_alpha_up", "b_alpha", "b_forget", "conv_c_w", "conv_d_w", "conv_d_b",
               "lru_b_r", "lru_b_i", "lru_lambda", "lru_w_r", "lru_w_i"):
        d[nm] = f(inp[nm])
    d["ln_final"] = f(np.asarray(inp["ln_final"]).reshape(1, D))
    return d


def run_cores(inp, n_cores, NS, prompt_of_core, cfg_extra=None):
    DEPTH = inp["w_in"].shape[0]
    seq = inp["x_prompt"].shape[1]
    cfg = dict(L=seq + 16, NS=NS, depth=DEPTH, n_pool=inp["cache_k"].shape[1], n_pages=inp["page_table"].shape[1])
    if cfg_extra:
        cfg.update(cfg_extra)
    nc = build(cfg)
    maps = [_core_inputs(inp, prompt_of_core[c], c * NS, NS, DEPTH) for c in range(n_cores)]
    res = run_bass_kernel_spmd(nc, maps, core_ids=list(range(n_cores)))
    return res.results, cfg


def assemble(results, cfg, n_prompt, n_cores):
    L, NS, DEPTH = cfg["L"], cfg["NS"], cfg["depth"]
    R = results
    cat_p = lambda nm, shp: np.stack([R[b][nm].reshape(shp) for b in range(n_prompt)], axis=0)
    y_prompt = cat_p("y_prompt", (L - 16, D))
    y_sample = np.concatenate([R[c]["y_sample"].reshape(NS, 4, D) for c in range(n_cores)], axis=0)
    stp = lambda nm, shp: np.stack([R[b][nm].reshape((DEPTH,) + shp) for b in range(n_prompt)], axis=1)
    sts = lambda nm, shp: np.concatenate([R[c][nm].reshape((DEPTH, NS) + shp) for c in range(n_cores)], axis=1)
    return (y_prompt, y_sample,
            stp("k_prompt", (L, 4, 64)), stp("v_prompt", (L, 4, 64)), stp("logf_prompt", (L, 4)),
            stp("gla_prompt", (4, 32, 64)), stp("conv_c_prompt", (2, 256)), stp("rglru_h_prompt", (256,)),
            stp("conv_d_prompt", (3, 256)),
            sts("k_sample", (4, 4, 64)), sts("v_sample", (4, 4, 64)), sts("logf_sample", (4, 4)),
            sts("gla_sample", (4, 32, 64)), sts("conv_c_sample", (2, 256)), sts("rglru_h_sample", (256,)),
            sts("conv_d_sample", (3, 256)))


def kernel(**inputs):
    n_cores = 8
    nb = inputs["x_prompt"].shape[0]
    ns_total = inputs["x_sample"].shape[0]
    NS = ns_total // n_cores
    prompt_of_core = [c if c < nb else 0 for c in range(n_cores)]
    results, cfg = run_cores(inputs, n_cores, NS, prompt_of_core)
    outs = assemble(results, cfg, nb, n_cores)
    return tuple(np.ascontiguousarray(o.astype(np.float32)) for o in outs)
```
